# Optimizing a Trainium2 kernel written in Bass

```python
import jax, jax.numpy as jnp
from jax import lax
import numpy as np

D_MODEL = 1024
BATCH = 4
SEQ = 4096
DEPTH = 1

CHUNK = 64
M_HEADS = 4
M_HEAD_DIM = D_MODEL // M_HEADS
M_WIDTH = M_HEADS * M_HEAD_DIM
CONV_K = 4
SGU_BLOCK = 128
SGU_GROUPS = 4
SGU_WIDTH = D_MODEL
SGU_GROUP_DIM = SGU_WIDTH // SGU_GROUPS
D_FF = 2816
PROJ_SIZES = (M_WIDTH, M_WIDTH, M_WIDTH, M_WIDTH, M_HEADS, M_HEADS, 2 * SGU_WIDTH, D_MODEL, D_MODEL)
N_PROJ = 4 * M_WIDTH + 2 * M_HEADS + 2 * SGU_WIDTH + 2 * D_MODEL
RMS_EPS = 1e-6
LN_EPS = 1e-5

kernel_name = "hybrid_mlstm_sgu_macaron_block"


def rmsnorm(x, g):
    x32 = x.astype(jnp.float32)
    y = x32 * lax.rsqrt(jnp.mean(x32 * x32, axis=-1, keepdims=True) + RMS_EPS)
    return (y * g.astype(jnp.float32)).astype(x.dtype)


def layernorm32(x, eps):
    x32 = x.astype(jnp.float32)
    xc = x32 - jnp.mean(x32, axis=-1, keepdims=True)
    return xc * lax.rsqrt(jnp.mean(xc * xc, axis=-1, keepdims=True) + eps)


def swiglu(x, w_in, w_out):
    gate, up = jnp.split(x @ w_in, 2, axis=-1)
    return (jax.nn.silu(gate) * up) @ w_out


def causal_depthwise_conv(x, w, b):
    y = lax.conv_general_dilated(
        x, w[:, None, :].astype(x.dtype), window_strides=(1,),
        padding=[(CONV_K - 1, 0)], dimension_numbers=('NWC', 'WIO', 'NWC'),
        feature_group_count=x.shape[-1])
    return y + b.astype(x.dtype)


def mlstm_chunkwise(q, k, v, log_i, log_f):
    bsz, nh, seqlen, dh = q.shape
    nc = seqlen // CHUNK

    def to_chunks(t):
        return jnp.moveaxis(t.reshape(bsz, nh, nc, CHUNK, *t.shape[3:]), 2, 0)

    causal = jnp.tril(jnp.ones((CHUNK, CHUNK), dtype=bool))

    def step(carry, xs):
        c_state, n_state, m_state = carry
        qc, kc, vc, lic, lfc = xs
        b = jnp.cumsum(lfc, axis=-1)
        a_inter = b + m_state[..., None]
        d = jnp.where(causal, b[..., :, None] - b[..., None, :] + lic[..., None, :], -jnp.inf)
        m_t = jnp.maximum(a_inter, jnp.max(d, axis=-1))
        w_inter = jnp.exp(a_inter - m_t)
        scores = jnp.einsum('bhtd,bhsd->bhts', qc, kc) * jnp.exp(d - m_t[..., None])
        num = (w_inter[..., None] * jnp.einsum('bhed,bhtd->bhte', c_state, qc)
               + jnp.einsum('bhts,bhse->bhte', scores, vc))
        den = w_inter * jnp.einsum('bhd,bhtd->bht', n_state, qc) + jnp.sum(scores, axis=-1)
        h = num / jnp.maximum(jnp.abs(den), jnp.exp(-m_t))[..., None]
        b_last = b[..., -1]
        g = b_last[..., None] - b + lic
        m_new = jnp.maximum(b_last + m_state, jnp.max(g, axis=-1))
        decay = jnp.exp(b_last + m_state - m_new)
        wk = jnp.exp(g - m_new[..., None])
        c_new = decay[..., None, None] * c_state + jnp.einsum('bhs,bhse,bhsd->bhed', wk, vc, kc)
        n_new = decay[..., None] * n_state + jnp.einsum('bhs,bhsd->bhd', wk, kc)
        return (c_new, n_new, m_new), h

    init = (jnp.zeros((bsz, nh, dh, dh), jnp.float32),
            jnp.zeros((bsz, nh, dh), jnp.float32),
            jnp.zeros((bsz, nh), jnp.float32))
    xs = (to_chunks(q), to_chunks(k), to_chunks(v), to_chunks(log_i), to_chunks(log_f))
    _, hs = lax.scan(step, init, xs)
    return jnp.moveaxis(hs, 0, 2).reshape(bsz, nh, seqlen, dh)


def mlstm_branch(q_pre, k_pre, v_pre, o_pre, i_pre, f_pre, conv_w, conv_b, b_i, b_f, head_g):
    bsz, seqlen, _ = q_pre.shape
    qk = jax.nn.silu(causal_depthwise_conv(jnp.concatenate([q_pre, k_pre], axis=-1), conv_w, conv_b))
    q, k = jnp.split(qk, 2, axis=-1)

    def heads(t):
        return t.reshape(bsz, seqlen, M_HEADS, M_HEAD_DIM).transpose(0, 2, 1, 3).astype(jnp.float32)

    log_i = (i_pre + b_i).astype(jnp.float32).transpose(0, 2, 1)
    log_f = jax.nn.log_sigmoid((f_pre + b_f).astype(jnp.float32)).transpose(0, 2, 1)
    h = mlstm_chunkwise(heads(q), heads(k) * (M_HEAD_DIM ** -0.5), heads(v_pre), log_i, log_f)
    h = layernorm32(h, LN_EPS).transpose(0, 2, 1, 3).reshape(bsz, seqlen, M_WIDTH)
    return (jax.nn.sigmoid(o_pre.astype(jnp.float32)) * h * head_g).astype(q_pre.dtype)


def sgu_branch(z_pre, norm_g, norm_b, w_s, b_s):
    bsz, seqlen, _ = z_pre.shape
    u, v = jnp.split(jax.nn.gelu(z_pre, approximate=False), 2, axis=-1)
    v = (layernorm32(v, LN_EPS) * norm_g + norm_b).astype(z_pre.dtype)
    nb = seqlen // SGU_BLOCK
    v = v.reshape(bsz, nb, SGU_BLOCK, SGU_GROUPS, SGU_GROUP_DIM)
    chunk_id = jnp.arange(SGU_BLOCK) // CHUNK
    mask = chunk_id[:, None] >= chunk_id[None, :]
    w = jnp.where(mask, w_s, 0.0).astype(v.dtype)
    sv = jnp.einsum('gts,bnsgc->bntgc', w, v) + b_s.T[:, :, None].astype(v.dtype)
    return u * sv.reshape(bsz, seqlen, SGU_WIDTH)


def setup_inputs(seed: int = 0) -> dict:
    key = jax.random.key(seed)
    ks = jax.random.split(key, 20)
    L = DEPTH

    def nrm(k, shape, scale):
        return jax.random.normal(k, shape, jnp.float32) * scale

    return {
        'x': nrm(ks[0], (BATCH, SEQ, D_MODEL), 1.0),
        'ffn1_norm_g': 1.0 + nrm(ks[1], (L, D_MODEL), 0.05),
        'ffn1_w_in': nrm(ks[2], (L, D_MODEL, 2 * D_FF), D_MODEL ** -0.5),
        'ffn1_w_out': nrm(ks[3], (L, D_FF, D_MODEL), D_FF ** -0.5),
        'mix_norm_g': 1.0 + nrm(ks[4], (L, D_MODEL), 0.05),
        'w_mix_in': nrm(ks[5], (L, D_MODEL, N_PROJ), D_MODEL ** -0.5),
        'mlstm_conv_w': nrm(ks[6], (L, CONV_K, 2 * M_WIDTH), CONV_K ** -0.5),
        'mlstm_conv_b': nrm(ks[7], (L, 2 * M_WIDTH), 0.01),
        'mlstm_b_igate': nrm(ks[8], (L, M_HEADS), 0.1),
        'mlstm_b_fgate': jnp.linspace(3.0, 6.0, M_HEADS, dtype=jnp.float32) + nrm(ks[9], (L, M_HEADS), 0.1),
        'mlstm_head_norm_g': 1.0 + nrm(ks[10], (L, M_WIDTH), 0.05),
        'sgu_norm_g': 1.0 + nrm(ks[11], (L, SGU_WIDTH), 0.05),
        'sgu_norm_b': nrm(ks[12], (L, SGU_WIDTH), 0.01),
        'sgu_w_s': nrm(ks[13], (L, SGU_GROUPS, SGU_BLOCK, SGU_BLOCK), SGU_BLOCK ** -0.5),
        'sgu_b_s': 1.0 + nrm(ks[14], (L, SGU_GROUPS, SGU_BLOCK), 0.1),
        'w_mix_out': nrm(ks[15], (L, D_MODEL, D_MODEL), D_MODEL ** -0.5),
        'ffn2_norm_g': 1.0 + nrm(ks[16], (L, D_MODEL), 0.05),
        'ffn2_w_in': nrm(ks[17], (L, D_MODEL, 2 * D_FF), D_MODEL ** -0.5),
        'ffn2_w_out': nrm(ks[18], (L, D_FF, D_MODEL), D_FF ** -0.5),
        'final_norm_g': 1.0 + nrm(ks[19], (D_MODEL,), 0.05),
    }


def reference(x, ffn1_norm_g, ffn1_w_in, ffn1_w_out, mix_norm_g, w_mix_in, mlstm_conv_w, mlstm_conv_b,
              mlstm_b_igate, mlstm_b_fgate, mlstm_head_norm_g, sgu_norm_g, sgu_norm_b, sgu_w_s, sgu_b_s,
              w_mix_out, ffn2_norm_g, ffn2_w_in, ffn2_w_out, final_norm_g):
    split_idx = np.cumsum(PROJ_SIZES)[:-1].tolist()
    for l in range(DEPTH):
        x = x + 0.5 * swiglu(rmsnorm(x, ffn1_norm_g[l]), ffn1_w_in[l], ffn1_w_out[l])
        xn = rmsnorm(x, mix_norm_g[l])
        proj = xn @ w_mix_in[l]
        q_pre, k_pre, v_pre, o_pre, i_pre, f_pre, z_pre, gate_m, gate_s = jnp.split(proj, split_idx, axis=-1)
        h_m = mlstm_branch(q_pre, k_pre, v_pre, o_pre, i_pre, f_pre, mlstm_conv_w[l], mlstm_conv_b[l],
                           mlstm_b_igate[l], mlstm_b_fgate[l], mlstm_head_norm_g[l])
        h_s = sgu_branch(z_pre, sgu_norm_g[l], sgu_norm_b[l], sgu_w_s[l], sgu_b_s[l])
        merged = jax.nn.sigmoid(gate_m) * h_m + jax.nn.sigmoid(gate_s) * h_s
        x = x + merged @ w_mix_out[l]
        x = x + 0.5 * swiglu(rmsnorm(x, ffn2_norm_g[l]), ffn2_w_in[l], ffn2_w_out[l])
    return rmsnorm(x, final_norm_g)
```

```python
import contextlib
import numpy as np
import concourse.bass as bass
import concourse.mybir as mybir
from concourse.bass_utils import run_bass_kernel_spmd

F32 = mybir.dt.float32
BF16 = mybir.dt.bfloat16
AF = mybir.ActivationFunctionType
ALU = mybir.AluOpType

NCORES = 8
NT = 2048
G = 512
NG = NT // G
TPG = G // 128
NTILE = NT // 128
D = 1024
DC = 8
DFF = 2816
JF = DFF // 128
H = 4
NPROJ = 8200
OQ, OK_, OV, OO, OI, OF_, OZ, OGM, OGS = 0, 1024, 2048, 3072, 4096, 4100, 4104, 6152, 7176
RMS_EPS = 1e-6
LN_EPS = 1e-5
SPAY = 8 * 257 + 48

POOL_CONV = False
PROFILE_LABELS = None
STAGE = 99


class Sched:
    ENGS = ("pe", "act", "dve", "pool", "sp")

    def __init__(self, nc):
        self.nc = nc
        self.ops = []
        self.last_w = {}
        self.readers = {}
        self.count = {e: 0 for e in self.ENGS}
        self.ndma = {"sp": 8, "pool": 1, "act": 4}
        self.dma_rr = {e: 0 for e in self.ndma}
        self.dma_cnt = {}

    def add(self, eng, fn, reads=(), writes=(), dma=False, cc=False):
        deps = set()
        reads = list(reads)
        writes = list(writes)
        for k in reads:
            if k in self.last_w:
                deps.add(self.last_w[k])
        for k in writes:
            if k in self.last_w:
                deps.add(self.last_w[k])
            deps.update(self.readers.get(k, ()))
        if eng == "pool" and dma and "ccout" in self.last_w:
            deps.add(self.last_w["ccout"])
        if cc:
            for r in range(self.ndma["pool"]):
                n_ = self.dma_cnt.get(("pool", r), 0)
                if n_:
                    deps.add((("dma", "pool", r), 16 * n_))
        if dma:
            R = self.ndma[eng]
            r = self.dma_rr[eng] % R
            self.dma_rr[eng] += 1
            n = self.dma_cnt.get((eng, r), 0) + 1
            self.dma_cnt[(eng, r)] = n
            sem = ("dma", eng, r)
            tok = (sem, 16 * n)
            if n > 1:
                deps.add((sem, 16 * (n - 1)))
            inc = 16
        elif cc:
            sem = ("cc",)
            n = self.dma_cnt.get(sem, 0) + 1
            self.dma_cnt[sem] = n
            tok = (sem, n)
            inc = 1
        else:
            self.count[eng] += 1
            sem = ("eng", eng)
            tok = (sem, self.count[eng])
            inc = 1
        lab = ""
        if PROFILE_LABELS is not None:
            import sys as _s
            fr = _s._getframe(1)
            names = []
            while fr is not None and len(names) < 4:
                if fr.f_code.co_name not in ("pe", "act", "dve", "<lambda>", "add"):
                    names.append("%s:%d" % (fr.f_code.co_name, fr.f_lineno))
                fr = fr.f_back
            lab = "<".join(names) + "|" + str(getattr(self, "note", ""))
        self.ops.append(dict(eng=eng, fn=fn, deps=deps, tok=tok, inc=inc, lab=lab))
        for k in writes:
            self.last_w[k] = tok
            self.readers[k] = []
        for k in reads:
            if k not in writes:
                self.readers.setdefault(k, []).append(tok)
        return tok

    def fence(self, eng, fn, pred):
        keys = [k for k in set(self.last_w) | set(self.readers) if pred(k)]
        return self.add(eng, fn, reads=(), writes=keys + ["dummy"])

    def emit(self, final_waits=()):
        nc = self.nc
        sem_names = sorted({op["tok"][0] for op in self.ops}, key=str)
        with contextlib.ExitStack() as st:
            sems = {}
            for s in sem_names:
                sems[s] = st.enter_context(nc.semaphore("s_" + "_".join(str(x) for x in s)))
            block = st.enter_context(nc.Block())
            ops = self.ops

            def run(engname, eng):
                waited = {}
                for op in ops:
                    if op["eng"] != engname:
                        continue
                    need = {}
                    for (s, v) in op["deps"]:
                        if s == ("eng", "pe") and engname == "pe":
                            continue
                        if v > need.get(s, 0):
                            need[s] = v
                    for s, v in sorted(need.items(), key=str):
                        if waited.get(s, 0) >= v:
                            continue
                        eng.wait_ge(sems[s], v)
                        waited[s] = v
                    if PROFILE_LABELS is not None and engname in ("pe", "dve", "act"):
                        cnt = [0]

                        class _P:
                            def __getattr__(s_, nm):
                                f_ = getattr(eng, nm)

                                def w(*a, **kw):
                                    cnt[0] += 1
                                    return f_(*a, **kw)
                                return w
                        ins = op["fn"](_P())
                        PROFILE_LABELS.append((engname, op["lab"], cnt[0]))
                    else:
                        ins = op["fn"](eng)
                    ins.then_inc(sems[op["tok"][0]], op["inc"])
                if engname == "sp":
                    need = {}
                    for (s, v) in final_waits:
                        if v > need.get(s, 0):
                            need[s] = v
                    for s, v in sorted(need.items(), key=str):
                        eng.wait_ge(sems[s], v)

            @block.sync
            def _(e):
                run("sp", e)

            @block.scalar
            def _(e):
                run("act", e)

            @block.vector
            def _(e):
                run("dve", e)

            @block.tensor
            def _(e):
                run("pe", e)

            @block.gpsimd
            def _(e):
                run("pool", e)


def build_program(stage=99):
    nc = bass.Bass("TRN2", target_bir_lowering=False)
    S = Sched(nc)

    def dram_in(name, shape):
        return nc.dram_tensor(name, list(shape), F32, kind="ExternalInput")

    x_d = dram_in("x", [NT, D])
    w_d = {
        "f1i": dram_in("ffn1_w_in", [D, 2 * DFF]), "f1o": dram_in("ffn1_w_out", [DFF, D]),
        "f2i": dram_in("ffn2_w_in", [D, 2 * DFF]), "f2o": dram_in("ffn2_w_out", [DFF, D]),
        "mi": dram_in("w_mix_in", [D, NPROJ]), "mo": dram_in("w_mix_out", [D, D]),
    }
    gcols_d = dram_in("gcols", [128, 24])
    convp_d = dram_in("convp", [128, 16 * 5])
    bif_d = dram_in("bif", [128, 8])
    bs_d = dram_in("bs", [128, 4])
    wsT_d = dram_in("wsT", [128, 4 * 128])
    cst_d = dram_in("cst", [128, 3 * 128])
    sel_d = dram_in("sel", [128, 1])
    bc_d = {k: dram_in(k, [128, D]) for k in ("headg_bc", "sgug_bc", "sgub_bc", "fing_bc")}
    y_d = nc.dram_tensor("y", [NT, D], F32, kind="ExternalOutput")

    cc_in = nc.dram_tensor("cc_in", [128, SPAY], F32)
    cc_out = nc.dram_tensor("cc_out", [256, SPAY], F32)

    JGN = [2, 2, 2, 2, 2, 1, 2, 2, 2, 2, 2, 1]
    JG0 = [0, 2, 4, 6, 8, 10, 11, 13, 15, 17, 19, 21]
    JON = [4, 4, 3, 4, 4, 3]
    JO0 = [0, 4, 8, 11, 15, 19]
    SLOT = 2048
    slabs = {}

    def def_slab(name, src, a, w, sc, idx):
        assert a * w <= SLOT, (name, a, w)
        slabs[name] = dict(src=src, a=a, w=w, dst=sc[idx][:, 0:a * w], key=("wsc", name))

    for f in ("f1", "f2"):
        wi = w_d[f + "i"].ap().rearrange("(dc p) c -> p dc c", p=128)
        wo = w_d[f + "o"].ap().rearrange("(j p) c -> p j c", p=128)
        sc_i = nc.dram_tensor(f + "i_sc", [24, 128, SLOT], BF16)
        sc_o = nc.dram_tensor(f + "o_sc", [12, 128, SLOT], BF16)
        for s in range(12):
            j0, nj = JG0[s], JGN[s]
            def_slab(f"{f}g{s}", wi[:, :, j0 * 128:(j0 + nj) * 128], 8, nj * 128, sc_i, 2 * s)
            def_slab(f"{f}u{s}", wi[:, :, DFF + j0 * 128:DFF + (j0 + nj) * 128], 8, nj * 128, sc_i, 2 * s + 1)
        for hf in range(2):
            for jg in range(6):
                j0, nj = JO0[jg], JON[jg]
                def_slab(f"{f}o{hf}_{jg}", wo[:, j0:j0 + nj, hf * 512:(hf + 1) * 512], nj, 512,
                         sc_o, hf * 6 + jg)
    wm = w_d["mi"].ap().rearrange("(dc p) c -> p dc c", p=128)
    sc_m = nc.dram_tensor("mi_sc", [33, 128, SLOT], BF16)
    mi_names = []
    for nm, off, n in (("q", OQ, 2), ("k", OK_, 2)):
        for s in range(n):
            for hx, sfx in enumerate("ab"):
                c0 = off + s * 512 + hx * 256
                def_slab(f"m{nm}{s}{sfx}", wm[:, :, c0:c0 + 256], 8, 256, sc_m, len(mi_names))
                mi_names.append(f"m{nm}{s}{sfx}")
    for nm, off, n in (("v", OV, 2), ("o", OO, 2), ("z", OZ, 4), ("gm", OGM, 2), ("gs", OGS, 2)):
        for s in range(n):
            for hx, sfx in enumerate("ab"):
                def_slab(f"m{nm}{s}{sfx}", wm[:, hx * 4:hx * 4 + 4, off + s * 512:off + (s + 1) * 512], 4, 512, sc_m,
                         len(mi_names))
                mi_names.append(f"m{nm}{s}{sfx}")
    def_slab("mif", wm[:, :, OI:OI + 8], 8, 8, sc_m, 32)
    wmo = w_d["mo"].ap().rearrange("(cc p) c -> p cc c", p=128)
    sc_mo = nc.dram_tensor("mo_sc", [4, 128, SLOT], BF16)
    for hf in range(2):
        for hx, sfx in enumerate("ab"):
            def_slab(f"wo{hf}{sfx}", wmo[:, hx * 4:hx * 4 + 4, hf * 512:(hf + 1) * 512], 4, 512, sc_mo, hf * 2 + hx)

    def ffn_order(f):
        o = []
        for jh in range(2):
            for s in range(6 * jh, 6 * jh + 6):
                o += [f"{f}g{s}", f"{f}u{s}"]
            o += [f"{f}o{hf}_{jg}" for hf in range(2) for jg in range(3 * jh, 3 * jh + 3)]
        return o

    def ab(names):
        return [n + x for n in names for x in "ab"]
    conv_order = ffn_order("f1")
    conv_order += ab(["mk0", "mk1", "mv0", "mv1"]) + ["mif"] + ab(["mq0", "mq1", "mo0", "mo1", "mz0", "mz1", "mz2", "mz3",
                                                                  "mgm0", "mgm1", "mgs0", "mgs1", "wo0", "wo1"])
    conv_order = list(dict.fromkeys(conv_order))
    conv_order += ffn_order("f2")
    assert set(conv_order) == set(slabs)

    def sb(name, shape, dt):
        return nc.alloc_sbuf_tensor("sb_" + name, list(shape), dt)

    x_sb = sb("x_sb", [128, NTILE, D], F32)
    NS = 8
    ring = [sb(f"ring{i}", [128, SLOT], BF16) for i in range(NS)]
    wif = sb("wif", [128, 64], BF16)
    xnT = sb("xnT", [128, DC, G], BF16)
    St = sb("St", [128, 8, 257], F32)
    Stb = sb("Stb", [128, 8, 257], BF16)
    bcb = [sb("bcb0", [128, D], F32)]
    gcols = sb("gcols", [128, 24], F32)
    convp = sb("convp", [128, 16, 5], F32)
    bif = sb("bif", [128, 8], F32)
    bs_sb = sb("bs_sb", [128, 4], F32)
    cst = sb("cst", [128, 3, 128], F32)
    ident = sb("ident", [128, 128], BF16)
    sel = sb("sel", [128, 1], F32)
    WmT = sb("WmT", [128, 4, 128], BF16)
    hist = sb("hist", [128, 16, 3], F32)
    stats = sb("stats", [128, 4, 12], F32)
    mv = sb("mv", [128, 4, 2], F32)
    msq = sb("msq", [128, 4], F32)
    rstd = sb("rstd", [128, 4], F32)
    stats2 = sb("stats2", [128, 4, 12], F32)
    mv2 = sb("mv2", [128, 4, 2], F32)
    msq2 = sb("msq2", [128, 4], F32)
    rstd2 = sb("rstd2", [128, 4], F32)
    stats3 = sb("stats3", [128, 4, 12], F32)
    mv3 = sb("mv3", [128, 4, 2], F32)
    msq3 = sb("msq3", [128, 4], F32)
    rstd3 = sb("rstd3", [128, 4], F32)
    gpre = sb("gpre", [128, TPG, 8], F32)
    sm = sb("sm", [128, 64], F32)
    lfb = sb("lfb", [128, TPG, 4], F32)
    dummy = sb("dummy", [128, 2], F32)
    ARENA = 82016
    arena = sb("arena", [128, ARENA // 2], BF16)
    apos = [0]

    def carve(shape, dt, base=None):
        n = int(np.prod(shape[1:]))
        nb = n * (4 if dt == F32 else 2)
        if base is not None:
            apos[0] = base
        off = apos[0]
        apos[0] = off + ((nb + 31) // 32) * 32
        assert apos[0] <= ARENA, (apos[0], ARENA)
        v = arena[:, off // 2:off // 2 + nb // 2]
        if dt == F32:
            v = v.bitcast(F32)
        if len(shape) == 3:
            v = v.rearrange("p (a b) -> p a b", a=shape[1])
        elif len(shape) == 4:
            v = v.rearrange("p (a b c) -> p a b c", a=shape[1], b=shape[2])
        return v

    qT = carve([128, 8, G], BF16, base=0)
    kT = carve([128, 8, G], BF16)
    v1 = carve([128, TPG, H, 258], BF16)
    ogm = carve([128, TPG, D], BF16)
    ugs = carve([128, TPG, D], BF16)
    vs = carve([128, TPG, D], BF16)
    ktok = carve([128, D], BF16)
    EB = carve([128, 4, 128], F32)
    qs = carve([128, 8, 128], BF16)
    PT = carve([128, 4, 128], BF16)
    v1w = carve([128, H, 258], BF16)
    hh = carve([128, D], F32)
    mgb = carve([128, D], BF16)
    mT = carve([128, 8, 128], BF16)
    XB = apos[0]
    hT = carve([128, 11, G], BF16, base=XB)
    sg = carve([128, 2, G], F32)
    pre = carve([128, 2, 516], F32, base=XB)
    acc = carve([128, 2, G], F32)
    tg = carve([128, 2, G], F32)
    xch = carve([128, SPAY], F32, base=XB)
    apos[0] = XB + 15360
    xs_bf = sb("xs_bf", [128, D], BF16)

    ps = [nc.alloc_psum_tensor(f"ps{i}", [128, 512], F32) for i in range(8)]
    psb = [p.bitcast(BF16) for p in ps]

    def PS(b):
        return ("ps", b)

    is_arena = lambda k: isinstance(k, tuple) and len(k) > 1 and k[0] == "A" and k[1] in ("hT", "sg", "pre", "acc", "tg", "xch")

    def A(*k):
        return ("A",) + tuple(k)

    ring_pos = [0]

    def load_slab(name):
        sl = slabs[name]
        slot = ring_pos[0] % NS
        ring_pos[0] += 1
        n = sl["a"] * sl["w"]
        S.add("sp", lambda e, slot=slot, n=n, sl=sl: e.dma_start(out=ring[slot][:, 0:n], in_=sl["dst"]),
              reads=[sl["key"]], writes=[("ring", slot)], dma=True)
        return ring[slot][:, 0:n].rearrange("p (a w) -> p a w", a=sl["a"]), ("ring", slot)

    def dve(fn, reads=(), writes=()):
        return S.add("dve", fn, reads, writes)

    def act(fn, reads=(), writes=()):
        return S.add("act", fn, reads, writes)

    def pe(fn, reads=(), writes=()):
        return S.add("pe", fn, reads, writes)

    S.add("pool", lambda e: e.memset(St[:], 0.0), writes=["St"])
    S.add("pool", lambda e: e.memset(hist[:], 0.0), writes=["hist"])
    S.add("pool", lambda e: e.memset(dummy[:], 0.0), writes=["dummy"])
    eps_t = sb("eps_t", [128, 2], F32)
    S.add("pool", lambda e: e.memset(eps_t[:, 0:1], RMS_EPS), writes=["eps"])
    S.add("pool", lambda e: e.memset(eps_t[:, 1:2], LN_EPS), writes=["eps"])
    def emit_conv(names, extra_reads=()):
        for name in names:
            sl = slabs[name]
            S.add("pool", lambda e, sl=sl: e.dma_start(
                out=sl["dst"].rearrange("p (a w) -> p a w", a=sl["a"]), in_=sl["src"]),
                reads=list(extra_reads), writes=[sl["key"]], dma=True)
    n_early = conv_order.index("mo0a")
    n_mid = conv_order.index("f2g0")
    def load_x(g):
        S.add("sp", lambda e, g=g: e.dma_start(
            out=x_sb[:, g * TPG:(g + 1) * TPG, :],
            in_=x_d.ap()[g * G:(g + 1) * G, :].rearrange("(i p) d -> p i d", p=128)),
            writes=[("x", g * TPG + i) for i in range(TPG)], dma=True)
    load_x(0)
    emit_conv(conv_order[:2], extra_reads=[("x", 0)])
    emit_conv(conv_order[2:n_early])
    for g in range(1):
        if g == 0:
            for dst, src, key in ((gcols, gcols_d, "gcols"), (cst, cst_d, "cst"), (bif, bif_d, "bif"),
                                  (convp, convp_d, "convp"), (bs_sb, bs_d, "bs"), (sel, sel_d, "sel")):
                d2 = dst[:] if len(dst.shape) == 2 else dst[:].rearrange("p a b -> p (a b)")
                S.add("sp", lambda e, d2=d2, src=src: e.dma_start(out=d2, in_=src.ap()), writes=[key], dma=True)
    dve(lambda e: e.tensor_copy(out=ident[:], in_=cst[:, 0, :]), reads=["cst"], writes=["ident"])
    S.add("sp", lambda e: e.dma_start(out=hh[:, 0:512], in_=wsT_d.ap()), writes=[A("hh", 0), A("hh", 1)], dma=True)
    dve(lambda e: e.tensor_tensor(out=WmT[:], in0=hh[:, 0:512].rearrange("p (g t) -> p g t", g=4),
                                  in1=cst[:, 2:3, :].broadcast_to([128, 4, 128]), op=ALU.mult),
        reads=[A("hh", 0), A("hh", 1), "cst"], writes=["WmT"])

    def group_rstd(tiles, eps, out_ap, src_fn, nper, keys_r):
        n = len(tiles)
        for ii in range(n):
            for c in range(nper):
                dve(lambda e, ii=ii, c=c: e.bn_stats(stats[:, ii, c * 6:(c + 1) * 6], src_fn(ii, c)),
                    reads=keys_r(ii), writes=[("stats", ii, c)])
            dve(lambda e, ii=ii: e.bn_aggr(mv[:, ii, :], stats[:, ii, 0:6 * nper]),
                reads=[("stats", ii, c) for c in range(nper)], writes=[("mv", ii)])
        return n

    def rms_rstd(n):
        dve(lambda e: e.scalar_tensor_tensor(msq[:, 0:n], mv[:, 0:n, 0], 1.0, mv[:, 0:n, 0], op0=ALU.mult, op1=ALU.mult),
            reads=[("mv", i) for i in range(n)], writes=["msq"])
        dve(lambda e: e.tensor_tensor(out=msq[:, 0:n], in0=msq[:, 0:n], in1=mv[:, 0:n, 1], op=ALU.add),
            reads=["msq"] + [("mv", i) for i in range(n)], writes=["msq"])
        act(lambda e: e.activation(rstd[:, 0:n], msq[:, 0:n], AF.Ln, bias=eps_t[:, 0:1]), reads=["msq", "eps"], writes=["rstd"])
        act(lambda e: e.activation(rstd[:, 0:n], rstd[:, 0:n], AF.Exp, scale=-0.5), reads=["rstd"], writes=["rstd"])


    def norm_pre(g, alt=False):
        st_, mv_, ms_, rs_, sfx = [(stats, mv, msq, rstd, ""), (stats2, mv2, msq2, rstd2, "2"),
                                   (stats3, mv3, msq3, rstd3, "3")][int(alt)]
        n = TPG
        for ii in range(n):
            for c in range(2):
                dve(lambda e, ii=ii, c=c: e.bn_stats(st_[:, ii, c * 6:(c + 1) * 6], x_sb[:, g * TPG + ii, c * 512:(c + 1) * 512]),
                    reads=[("x", g * TPG + ii)], writes=[("stats" + sfx, ii, c)])
            dve(lambda e, ii=ii: e.bn_aggr(mv_[:, ii, :], st_[:, ii, 0:12]),
                reads=[("stats" + sfx, ii, c) for c in range(2)], writes=[("mv" + sfx, ii)])
        dve(lambda e: e.scalar_tensor_tensor(ms_[:, 0:n], mv_[:, 0:n, 0], 1.0, mv_[:, 0:n, 0], op0=ALU.mult, op1=ALU.mult),
            reads=[("mv" + sfx, i) for i in range(n)], writes=["msq" + sfx])
        dve(lambda e: e.tensor_tensor(out=ms_[:, 0:n], in0=ms_[:, 0:n], in1=mv_[:, 0:n, 1], op=ALU.add),
            reads=["msq" + sfx] + [("mv" + sfx, i) for i in range(n)], writes=["msq" + sfx])
        act(lambda e: e.activation(rs_[:, 0:n], ms_[:, 0:n], AF.Ln, bias=eps_t[:, 0:1]), reads=["msq" + sfx, "eps"], writes=["rstd" + sfx])
        act(lambda e: e.activation(rs_[:, 0:n], rs_[:, 0:n], AF.Exp, scale=-0.5), reads=["rstd" + sfx], writes=["rstd" + sfx])

    def norm_T_gen(g, gsel, dst=None, kn="xnT", banks=(0, 1), alt=False, pre=True):
        dst = xnT if dst is None else dst
        rs_, sfx = [(rstd, ""), (rstd2, "2"), (rstd3, "3")][int(alt)]
        if pre:
            norm_pre(g, alt)
            yield
            yield
        for ii in range(TPG):
            gi = g * TPG + ii
            act(lambda e, ii=ii, gi=gi: e.activation(xs_bf[:], x_sb[:, gi, :], AF.Copy, scale=rs_[:, ii:ii + 1]),
                reads=[("x", gi), "rstd" + sfx], writes=["xs_bf"])
            yield
            b = banks[ii % 2]
            pe(lambda e, b=b: [e.transpose(psb[b][:, dc * 128:(dc + 1) * 128], xs_bf[:, dc * 128:(dc + 1) * 128],
                                           ident[:]) for dc in range(DC)][-1],
               reads=["xs_bf", "ident"], writes=[PS(b)] + (PSD_KEYS if b == 7 else []))
            dve(lambda e, b=b, ii=ii: e.tensor_tensor(
                out=dst[:, :, ii * 128:(ii + 1) * 128],
                in0=psb[b][:, 0:1024].rearrange("p (d t) -> p d t", d=DC),
                in1=gcols[:, gsel * 8:(gsel + 1) * 8][:, :, None].broadcast_to([128, DC, 128]), op=ALU.mult),
                reads=[PS(b), "gcols"], writes=[(kn, ii)])
            yield

    def norm_T(g, gsel):
        for _ in norm_T_gen(g, gsel):
            pass

    PSD_KEYS = ([("psD", "bc")] + [("psD", "den", h) for h in range(H)] + [("psD", "n", h) for h in range(H)]
                + [("psD", "qh", si) for si in range(4)])
    xnT_all = [("xnT", ii) for ii in range(TPG)]

    def pump(bg, n):
        if bg is None:
            return
        if isinstance(bg, list):
            for _ in range(n):
                while bg:
                    if next(bg[0], "END") == "END":
                        bg.pop(0)
                    else:
                        break
                if not bg:
                    return
            return
        for _ in range(n):
            if next(bg, "END") == "END":
                return

    def fence_x():
        S.fence("dve", lambda e: e.memset(dummy[:, 0:1], 0.0), is_arena)

    def ffn(g, f, gsel, bg=None, rate=1, rateB=None, skip_norm=False, hoist=None, drain=True, stats_ready=False):
        rateB = rate if rateB is None else rateB
        if f == "f1" and g + 1 < NG:
            load_x(g + 1)
        fence_x()
        if not skip_norm:
            if stats_ready:
                for _ in norm_T_gen(g, gsel, alt=2, pre=False):
                    pass
            else:
                norm_T(g, gsel)
        hgen = None
        for jh in range(2):
            if jh == 1 and hoist is not None:
                hoist[0]()
            jl = 0
            for s in range(6 * jh, 6 * jh + 6):
                wg, kg = load_slab(f"{f}g{s}")
                wu, ku = load_slab(f"{f}u{s}")
                for jj in range(JGN[s]):
                    bA, bB = (0, 1) if jl % 2 == 0 else (2, 3)

                    def mm(e, wg=wg, wu=wu, jj=jj, bA=bA, bB=bB):
                        for dc in range(DC):
                            e.matmul(ps[bA][:], wg[:, dc, jj * 128:(jj + 1) * 128], xnT[:, dc, :],
                                     start=(dc == 0), stop=(dc == DC - 1))
                        for dc in range(DC):
                            r = e.matmul(ps[bB][:], wu[:, dc, jj * 128:(jj + 1) * 128], xnT[:, dc, :],
                                         start=(dc == 0), stop=(dc == DC - 1))
                        return r
                    S.note = f"g{g}{f}A jh{jh} s{s} jj{jj}"
                    pe(mm, reads=[kg, ku] + xnT_all, writes=[PS(bA), PS(bB)])
                    act(lambda e, bA=bA, jl=jl: e.activation(sg[:, jl % 2, :], ps[bA][:], AF.Silu),
                        reads=[PS(bA)], writes=[A("sg", jl % 2)])
                    dve(lambda e, bB=bB, jl=jl: e.tensor_tensor(out=hT[:, jl, :], in0=sg[:, jl % 2, :], in1=ps[bB][:],
                                                                op=ALU.mult),
                        reads=[A("sg", jl % 2), PS(bB)], writes=[A("hT", jl)])
                    jl += 1
                    pump(bg, rate)
            njh = jl
            for hf in range(2):
                jl = 0
                for s in range(3 * jh, 3 * jh + 3):
                    wo, ko = load_slab(f"{f}o{hf}_{s}")
                    nj = JON[s]

                    def mm(e, wo=wo, jl=jl, nj=nj, njh=njh):
                        r = None
                        for jj in range(nj):
                            jx = jl + jj
                            for i in range(TPG):
                                r = e.matmul(ps[i][:], hT[:, jx, i * 128:(i + 1) * 128], wo[:, jj, :],
                                             start=(jx == 0), stop=(jx == njh - 1))
                        return r
                    S.note = f"g{g}{f}B jh{jh} hf{hf} s{s}"
                    pe(mm, reads=[ko] + [A("hT", jl + jj) for jj in range(nj)], writes=[PS(i) for i in range(TPG)])
                    jl += nj
                    if s == 3 * jh + 2:
                        for i in range(TPG):
                            gi = g * TPG + i
                            dve(lambda e, i=i, gi=gi, hf=hf: e.scalar_tensor_tensor(
                                x_sb[:, gi, hf * 512:(hf + 1) * 512], ps[i][:], 0.5, x_sb[:, gi, hf * 512:(hf + 1) * 512],
                                op0=ALU.mult, op1=ALU.add),
                                reads=[PS(i), ("x", gi)], writes=[("x", gi)])
                    if jh == 1 and hoist is not None:
                        if hgen is None:
                            hgen = hoist[1]()
                        pump(hgen, 1)
                    if not (jh == 1 and hf == 1 and s >= 3 * jh + 1):
                        pump(bg, rateB)
        if hgen is not None:
            pump(hgen, 10000)
        if drain:
            pump(bg, 10000)

    xnT1 = ogm.rearrange("p a b -> p (a b)").rearrange("p (d t) -> p d t", d=DC)
    pre1 = ugs.rearrange("p a b -> p (a b)")[:, 0:2 * 516 * 2].bitcast(F32).rearrange("p (a b) -> p a b", a=2)
    acc1 = vs.rearrange("p a b -> p (a b)")[:, 0:2 * G * 2].bitcast(F32).rearrange("p (a b) -> p a b", a=2)

    def conv_chunks(g, names, qk_off, dstT, phase):
        xin, kn = (xnT1, "xnT1") if phase == 1 else (xnT, "xnT")
        pr, ac = (pre1, acc1) if phase == 1 else (pre, acc)
        kp, ka = ("pre1", "acc1") if phase == 1 else ("pre", "acc")
        banks = (4, 5, 6, 7) if phase == 1 else (0, 1, 2, 3)
        xall = [(kn, ii) for ii in range(TPG)]
        c = 0
        pend = []
        for nm in ab(names):
            w, kw = load_slab(nm)
            for cc in range(2):
                ch = qk_off + c
                b = banks[c % 4]
                pe(lambda e, w=w, cc=cc, b=b: [e.matmul(ps[b][:], w[:, dc, cc * 128:(cc + 1) * 128], xin[:, dc, :],
                                                        start=(dc == 0), stop=(dc == DC - 1)) for dc in range(DC)][-1],
                   reads=[kw] + xall, writes=[PS(b)] + (PSD_KEYS if b == 7 else []))
                pb = c % 2
                veng = "pool" if (phase == 2 and POOL_CONV and c % 2 == 1) else "dve"
                act(lambda e, b=b, pb=pb: e.activation(pr[:, pb, 3:515], ps[b][:], AF.Copy),
                    reads=[PS(b)], writes=[A(kp, pb, "m")])
                S.add(veng, lambda e, pb=pb, ch=ch: e.tensor_copy(out=pr[:, pb, 0:3], in_=hist[:, ch, :]),
                      reads=[("hist", ch)], writes=[A(kp, pb, "h")])
                act(lambda e, b=b, pb=pb, ch=ch: e.activation(ac[:, pb, :], ps[b][:], AF.Identity, bias=convp[:, ch, 4:5],
                                                              scale=convp[:, ch, 3:4]),
                    reads=[PS(b), "convp"], writes=[A(ka, pb)])
                while pend:
                    pend.pop()()
                for jx in range(3):
                    S.add(veng, lambda e, pb=pb, ch=ch, jx=jx: e.scalar_tensor_tensor(
                        ac[:, pb, :], pr[:, pb, jx:jx + 512], convp[:, ch, jx:jx + 1], ac[:, pb, :], op0=ALU.mult, op1=ALU.add),
                        reads=[A(kp, pb, "m"), A(kp, pb, "h"), A(ka, pb)], writes=[A(ka, pb)])
                S.add(veng, lambda e, pb=pb, ch=ch: e.tensor_copy(out=hist[:, ch, :], in_=pr[:, pb, 512:515]),
                      reads=[A(kp, pb, "m")], writes=[("hist", ch)])
                pend.append(lambda c=c, pb=pb: act(lambda e: e.activation(dstT[:, c, :], ac[:, pb, :], AF.Silu),
                                                   reads=[A(ka, pb)], writes=[A("T", qk_off, c)]))
                c += 1
                yield
        while pend:
            pend.pop()()

    def tok_slab(g, nm, evac, phase=2):
        xin, kn = (xnT1, "xnT1") if phase == 1 else (xnT, "xnT")
        wa, kwa = load_slab(nm + "a")
        wb, kwb = load_slab(nm + "b")
        if phase == 1:
            base = 4
        else:
            tok_slab.flip ^= 1
            base = 4 * tok_slab.flip
        for i in range(TPG):
            b = base + i
            pe(lambda e, i=i, b=b: [e.matmul(ps[b][:], xin[:, dc, i * 128:(i + 1) * 128],
                                             (wa if dc < 4 else wb)[:, dc % 4, :],
                                             start=(dc == 0), stop=(dc == DC - 1)) for dc in range(DC)][-1],
               reads=[kwa, kwb, (kn, i)], writes=[PS(b)] + (PSD_KEYS if b == 7 else []))
            evac(i, b)
            yield
    tok_slab.flip = 0

    def load_bc(name):
        S.add("pool", lambda e: e.dma_start(out=bcb[0][:], in_=bc_d[name].ap()), writes=[("bcb", 0)], dma=True)

    def mixer_proj(g, phase, after_norm=None, skip_norm=False, stats_ready=False):
        xin, kn = (xnT1, "xnT1") if phase == 1 else (xnT, "xnT")
        if phase == 2:
            fence_x()
        dve(lambda e: e.memset(v1.rearrange("p a b c -> p (a b) c")[:, :, 256:258], 1.0), writes=[A("v1ones")])
        if skip_norm:
            pass
        elif phase == 1:
            yield from norm_T_gen(g, 1, dst=xnT1, kn="xnT1", banks=(4, 5))
        elif stats_ready:
            yield from norm_T_gen(g, 1, alt=2, pre=False)
        else:
            yield from norm_T_gen(g, 1)
        if after_norm is not None:
            after_norm()
        if phase == 2:
            yield from conv_chunks(g, ["mq0", "mq1"], 0, qT, phase)
        elif g == NG - 1:
            for si, nm in enumerate(ab(["mq0", "mq1"])):
                w, kw = load_slab(nm)
                pe(lambda e, w=w, si=si: [e.matmul(ps[7][:, 32 + (2 * si + cc) * 4:32 + (2 * si + cc) * 4 + 3],
                                                   w[:, dc, cc * 128:(cc + 1) * 128], xnT1[:, dc, G - 3:G],
                                                   start=(dc == 0), stop=(dc == DC - 1))
                                          for cc in range(2) for dc in range(DC)][-1],
                   reads=[kw] + [("xnT1", ii) for ii in range(TPG)], writes=[("psD", "qh", si)])
            dve(lambda e: e.tensor_copy(out=hist[:, 0:8, :],
                                        in_=ps[7][:, 32:64].rearrange("p (c f) -> p c f", f=4)[:, :, 0:3]),
                reads=[("psD", "qh", si) for si in range(4)], writes=[("hist", c) for c in range(8)])
            yield
        yield from conv_chunks(g, ["mk0", "mk1"], 8, kT, phase)
        for s in range(2):
            yield from tok_slab(g, f"mv{s}", lambda i, b, s=s: act(
                lambda e: e.activation(v1[:, i, 2 * s:2 * s + 2, 0:256], ps[b][:].rearrange("p (h e) -> p h e", h=2), AF.Copy),
                reads=[PS(b)], writes=[A("v1", i, s)]), phase)
        S.add("sp", lambda e: e.dma_start(out=wif[:], in_=slabs["mif"]["dst"]), reads=[slabs["mif"]["key"]],
              writes=["wif"], dma=True)
        wifv = wif[:].rearrange("p (a w) -> p a w", a=8)
        for i in range(TPG):
            pe(lambda e, i=i: [e.matmul(ps[4][:, i * 8:(i + 1) * 8], xin[:, dc, i * 128:(i + 1) * 128], wifv[:, dc, :],
                                        start=(dc == 0), stop=(dc == DC - 1)) for dc in range(DC)][-1],
               reads=["wif", (kn, i)], writes=[PS(4)])
            dve(lambda e, i=i: e.tensor_tensor(out=gpre[:, i, :], in0=ps[4][:, i * 8:(i + 1) * 8], in1=bif[:], op=ALU.add),
                reads=[PS(4), "bif"], writes=[("gpre", i)])
        yield
        zz = gpre[:, :, 4:8]
        dve(lambda e: e.scalar_tensor_tensor(lfb[:], zz, -1.0, zz, op0=ALU.mult, op1=ALU.max),
            reads=[("gpre", i) for i in range(TPG)], writes=["lfb"])
        act(lambda e: e.activation(lfb[:], lfb[:], AF.Exp, scale=-1.0), reads=["lfb"], writes=["lfb"])
        act(lambda e: e.activation(lfb[:], lfb[:], AF.Ln, bias=1.0), reads=["lfb"], writes=["lfb"])
        dve(lambda e: e.scalar_tensor_tensor(lfb[:], zz, 0.0, lfb[:], op0=ALU.min, op1=ALU.subtract),
            reads=[("gpre", i) for i in range(TPG)] + ["lfb"], writes=["lfb"])
        if phase == 1:
            return
        for s in range(4):
            dst = ugs if s < 2 else vs
            yield from tok_slab(g, f"mz{s}", lambda i, b, s=s, dst=dst: act(
                lambda e: e.activation(dst[:, i, (s % 2) * 512:(s % 2 + 1) * 512], ps[b][:], AF.Gelu),
                reads=[PS(b)], writes=[A("ugs" if s < 2 else "vs", i, s % 2)]))
        ln = sgu_ln(g)
        for s in range(2):
            def evo(i, b, s=s):
                act(lambda e: e.activation(ogm[:, i, s * 512:(s + 1) * 512], ps[b][:], AF.Sigmoid),
                    reads=[PS(b)], writes=[A("ogm", i, s)])
                next(ln, None)
            yield from tok_slab(g, f"mo{s}", evo)
        for s in range(2):
            def ev(i, b, s=s):
                t = (i + s) % 2
                act(lambda e: e.activation(tg[:, t, :], ps[b][:], AF.Sigmoid), reads=[PS(b)], writes=[A("tg", t)])
                dve(lambda e: e.tensor_tensor(out=ugs[:, i, s * 512:(s + 1) * 512], in0=ugs[:, i, s * 512:(s + 1) * 512],
                                              in1=tg[:, t, :], op=ALU.mult),
                    reads=[A("tg", t), A("ugs", i, s)], writes=[A("ugs", i, s)])
                next(ln, None)
            yield from tok_slab(g, f"mgs{s}", ev)
        for _ in ln:
            pass
        load_bc("headg_bc")
        for s in range(2):
            def ev(i, b, s=s):
                t = (i + s) % 2
                act(lambda e: e.activation(tg[:, t, :], ps[b][:], AF.Sigmoid), reads=[PS(b)], writes=[A("tg", t)])
                dve(lambda e: e.tensor_tensor(out=tg[:, t, :], in0=tg[:, t, :], in1=bcb[0][:, s * 512:(s + 1) * 512], op=ALU.mult),
                    reads=[A("tg", t), ("bcb", 0)], writes=[A("tg", t)])
                dve(lambda e: e.tensor_tensor(out=ogm[:, i, s * 512:(s + 1) * 512], in0=ogm[:, i, s * 512:(s + 1) * 512],
                                              in1=tg[:, t, :], op=ALU.mult),
                    reads=[A("tg", t), A("ogm", i, s)], writes=[A("ogm", i, s)])
            yield from tok_slab(g, f"mgm{s}", ev)

    def sgu_ln(g):
        load_bc("sgug_bc")
        for ii in range(TPG):
            for c in range(2):
                dve(lambda e, ii=ii, c=c: e.bn_stats(stats2[:, ii, c * 6:(c + 1) * 6], vs[:, ii, c * 512:(c + 1) * 512]),
                    reads=[A("vs", ii, 0), A("vs", ii, 1)], writes=[("stats2", ii, c)])
            dve(lambda e, ii=ii: e.bn_aggr(mv2[:, ii, :], stats2[:, ii, 0:12]),
                reads=[("stats2", ii, c) for c in range(2)], writes=[("mv2", ii)])
            yield
        act(lambda e: e.activation(rstd2[:, 0:TPG], mv2[:, 0:TPG, 1], AF.Ln, bias=eps_t[:, 1:2]),
            reads=[("mv2", i) for i in range(TPG)] + ["eps"], writes=["rstd2"])
        act(lambda e: e.activation(rstd2[:, 0:TPG], rstd2[:, 0:TPG], AF.Exp, scale=-0.5), reads=["rstd2"], writes=["rstd2"])
        for i in range(TPG):
            dve(lambda e, i=i: e.scalar_tensor_tensor(vs[:, i, :], vs[:, i, :], mv2[:, i, 0:1], bcb[0][:], op0=ALU.subtract, op1=ALU.mult),
                reads=[A("vs", i, 0), A("vs", i, 1), ("mv2", i), ("bcb", 0)], writes=[A("vs", i, 0), A("vs", i, 1)])
            yield
        load_bc("sgub_bc")
        for i in range(TPG):
            dve(lambda e, i=i: e.scalar_tensor_tensor(vs[:, i, :], vs[:, i, :], rstd2[:, i:i + 1], bcb[0][:], op0=ALU.mult, op1=ALU.add),
                reads=[A("vs", i, 0), A("vs", i, 1), "rstd2", ("bcb", 0)], writes=[A("vs", i, 0), A("vs", i, 1)])
            yield

    TA, TB, TC, TD = 4, 5, 6, 7

    def mixer_tile(g, i, phase):
        gi = g * TPG + i
        tc = slice(i * 128, (i + 1) * 128)
        li = gpre[:, i, 0:4]
        lf = lfb[:, i, :]
        dl, wk, wkd, den, rd = (sm[:, 4 * k:4 * k + 4] for k in range(5))
        pe(lambda e: e.matmul(ps[TD][:, 0:4], cst[:, 1, :], lf, start=True, stop=True), reads=["lfb", "cst"], writes=[("psD", "bc"), PS(TD)])
        pe(lambda e: [e.matmul(ps[TA][:, h * 128:(h + 1) * 128], lf[:, h:h + 1].broadcast_to([128, 128]), cst[:, 1, :],
                               start=True, stop=True) for h in range(H)][-1],
           reads=["lfb", "cst"], writes=[PS(TA)])
        pe(lambda e: [e.transpose(psb[TB][:, c * 128:(c + 1) * 128], kT[:, c, tc], ident[:]) for c in range(8)][-1],
           reads=[A("T", 8, c) for c in range(8)] + ["ident"], writes=[PS(TB)])
        act(lambda e: e.activation(EB[:].rearrange("p h t -> p (h t)"), ps[TA][:], AF.Exp), reads=[PS(TA)], writes=[A("EB")])
        dve(lambda e: e.tensor_tensor(out=dl, in0=li, in1=ps[TD][:, 0:4], op=ALU.subtract), reads=[("gpre", i), ("psD", "bc")], writes=["dl"])
        act(lambda e: e.activation(wk, dl, AF.Exp), reads=["dl"], writes=["wk"])
        dve(lambda e: e.tensor_tensor(out=wkd, in0=wk, in1=EB[:, :, 127], op=ALU.mult), reads=["wk", A("EB")], writes=["wkd"])
        act(lambda e: e.activation(ktok[:], psb[TB][:, 0:1024], AF.Copy), reads=[PS(TB)], writes=[A("ktok")])
        for h in range(H):
            act(lambda e, h=h: e.activation(v1w[:, h, 0:257], v1[:, i, h, 0:257], AF.Copy, scale=wkd[:, h:h + 1]),
                reads=[A("v1", i, h // 2), A("v1ones"), "wkd"], writes=[A("v1w", h)])
        if phase == 2:
            for h in range(H):
                dve(lambda e, h=h: e.scalar_tensor_tensor(
                    qs[:, 2 * h:2 * h + 2, :], qT[:, 2 * h:2 * h + 2, tc], 1.0 / 16.0,
                    EB[:, h:h + 1, :].broadcast_to([128, 2, 128]), op0=ALU.mult, op1=ALU.mult),
                    reads=[A("T", 0, 2 * h), A("T", 0, 2 * h + 1), A("EB")], writes=[A("qs", h)])
        yield

        def state_upd(h, bk=TA):
            def mm(e, h=h):
                for c in range(2):
                    e.matmul(ps[bk][:, c * 256:(c + 1) * 256], ktok[:, (h * 2 + c) * 128:(h * 2 + c + 1) * 128], v1w[:, h, 0:256],
                             start=True, stop=True)
                for c in range(2):
                    r = e.matmul(ps[TD][:, 16 + h * 2 + c:17 + h * 2 + c], ktok[:, (h * 2 + c) * 128:(h * 2 + c + 1) * 128],
                                 v1w[:, h, 256:257], start=True, stop=True)
                return r
            pe(mm, reads=[A("ktok"), A("v1w", h)], writes=[PS(bk), ("psD", "n", h)])
            dve(lambda e, h=h: e.scalar_tensor_tensor(
                St[:, 2 * h:2 * h + 2, 0:256], St[:, 2 * h:2 * h + 2, 0:256], EB[:, h, 127:128],
                ps[bk][:].rearrange("p (c e) -> p c e", c=2), op0=ALU.mult, op1=ALU.add),
                reads=[PS(bk), A("EB"), ("St", h)], writes=[("St", h)])
            dve(lambda e, h=h: e.scalar_tensor_tensor(
                St[:, 2 * h:2 * h + 2, 256], St[:, 2 * h:2 * h + 2, 256], EB[:, h, 127:128],
                ps[TD][:, 16 + 2 * h:18 + 2 * h], op0=ALU.mult, op1=ALU.add),
                reads=[("psD", "n", h), A("EB"), ("St", h)], writes=[("St", h)])

        def stb_copy(h):
            act(lambda e, h=h: e.activation(Stb[:, 2 * h:2 * h + 2, :], St[:, 2 * h:2 * h + 2, :], AF.Copy),
                reads=[("St", h)], writes=[("Stb", h)])

        if phase == 1:
            for h in (0, 2):
                state_upd(h, TA)
                state_upd(h + 1, TC)
                yield
            return

        def out_mm(h):
            ob = TB if h < 2 else TC
            oc = (h % 2) * 256

            def mm(e, h=h, ob=ob, oc=oc):
                for c in range(2):
                    e.matmul(ps[ob][:, oc:oc + 256], qs[:, h * 2 + c, :], Stb[:, h * 2 + c, 0:256], start=(c == 0), stop=False)
                e.matmul(ps[ob][:, oc:oc + 256], PT[:, h, :], v1[:, i, h, 0:256], start=False, stop=True)
                for c in range(2):
                    e.matmul(ps[TD][:, 8 + h:9 + h], qs[:, h * 2 + c, :], Stb[:, h * 2 + c, 256:257], start=(c == 0), stop=False)
                return e.matmul(ps[TD][:, 8 + h:9 + h], PT[:, h, :], v1[:, i, h, 256:257], start=False, stop=True)
            pe(mm, reads=[A("qs", h), ("Stb", h), A("PT", h), A("v1", i, h // 2), A("v1ones")], writes=[PS(ob), ("psD", "den", h)])

        pe(lambda e: [e.matmul(ps[TC][:, h * 128:(h + 1) * 128], kT[:, h * 2 + c, tc], qs[:, h * 2 + c, :],
                               start=(c == 0), stop=(c == 1)) for h in range(H) for c in range(2)][-1],
           reads=[A("T", 8, c) for c in range(8)] + [A("qs", h) for h in range(H)], writes=[PS(TC)])
        state_upd(0)
        for h in range(H):
            dve(lambda e, h=h: e.scalar_tensor_tensor(PT[:, h, :], ps[TC][:, h * 128:(h + 1) * 128], wk[:, h:h + 1],
                                                      cst[:, 1, :], op0=ALU.mult, op1=ALU.mult),
                reads=[PS(TC), "wk", "cst"], writes=[A("PT", h)])
        yield
        def finish_heads(h0):
            hs_ = slice(h0, h0 + 2)
            kr, kd = ("rd", h0), ("den", h0)
            dve(lambda e: e.tensor_copy(out=rd[:, hs_], in_=ps[TD][:, 8 + h0:10 + h0]),
                reads=[("psD", "den", h0), ("psD", "den", h0 + 1)], writes=[kr])
            dve(lambda e: e.scalar_tensor_tensor(den[:, hs_], rd[:, hs_], -1.0, rd[:, hs_], op0=ALU.mult, op1=ALU.max), reads=[kr], writes=[kd])
            dve(lambda e: e.tensor_scalar(den[:, hs_], den[:, hs_], 1.0, None, op0=ALU.max), reads=[kd], writes=[kd])
            dve(lambda e: e.reciprocal(rd[:, hs_], den[:, hs_]), reads=[kd], writes=[kr])
            for h in (h0, h0 + 1):
                ob = TB if h < 2 else TC
                oc = (h % 2) * 256
                act(lambda e, h=h, ob=ob, oc=oc: e.activation(hh[:, h * 256:(h + 1) * 256], ps[ob][:, oc:oc + 256], AF.Copy,
                                                              scale=rd[:, h:h + 1]),
                    reads=[PS(ob), kr], writes=[A("hh", h)])
                dve(lambda e, h=h: e.bn_stats(stats[:, h, 0:6], hh[:, h * 256:(h + 1) * 256]), reads=[A("hh", h)], writes=[("stats", h, 0)])
                dve(lambda e, h=h: e.bn_aggr(mv[:, h, :], stats[:, h, 0:6]), reads=[("stats", h, 0)], writes=[("mv", h)])

        out_mm(0)
        out_mm(1)
        state_upd(1)
        stb_copy(0)
        stb_copy(1)
        finish_heads(0)
        yield
        out_mm(2)
        out_mm(3)
        state_upd(2)
        finish_heads(2)
        yield
        state_upd(3)
        stb_copy(2)
        stb_copy(3)
        act(lambda e: e.activation(rstd[:, 0:H], mv[:, 0:H, 1], AF.Ln, bias=eps_t[:, 1:2]),
            reads=[("mv", h) for h in range(H)] + ["eps"], writes=["rstd"])
        act(lambda e: e.activation(rstd[:, 0:H], rstd[:, 0:H], AF.Exp, scale=-0.5), reads=["rstd"], writes=["rstd"])
        for h in range(H):
            dve(lambda e, h=h: e.tensor_scalar(hh[:, h * 256:(h + 1) * 256], hh[:, h * 256:(h + 1) * 256], mv[:, h, 0:1],
                                               rstd[:, h:h + 1], op0=ALU.subtract, op1=ALU.mult),
                reads=[A("hh", h), ("mv", h), "rstd"], writes=[A("hh", h)])
        dve(lambda e: e.tensor_tensor(out=hh[:], in0=hh[:], in1=ogm[:, i, :], op=ALU.mult),
            reads=[A("hh", h) for h in range(H)] + [A("ogm", i, 0), A("ogm", i, 1)], writes=[A("hh", h) for h in range(H)])
        yield
        pe(lambda e: [e.matmul(ps[TB if gq < 2 else TC][:, (gq % 2) * 256:(gq % 2 + 1) * 256], WmT[:, gq, :],
                               vs[:, i, gq * 256:(gq + 1) * 256], start=True, stop=True) for gq in range(4)][-1],
           reads=["WmT", A("vs", i, 0), A("vs", i, 1)], writes=[PS(TB), PS(TC)])
        EBf = mT.rearrange("p c t -> p (c t)").bitcast(F32)
        for gq in range(4):
            bq = TB if gq < 2 else TC
            dve(lambda e, gq=gq, bq=bq: e.scalar_tensor_tensor(
                EBf[:, (gq % 2) * 256:(gq % 2 + 1) * 256], ps[bq][:, (gq % 2) * 256:(gq % 2 + 1) * 256],
                bs_sb[:, gq:gq + 1], ugs[:, i, gq * 256:(gq + 1) * 256], op0=ALU.add, op1=ALU.mult),
                reads=[PS(bq), "bs", A("ugs", i, gq // 2)], writes=[A("mT")])
            dve(lambda e, gq=gq: e.tensor_tensor(out=mgb[:, gq * 256:(gq + 1) * 256], in0=hh[:, gq * 256:(gq + 1) * 256],
                                                 in1=EBf[:, (gq % 2) * 256:(gq % 2 + 1) * 256], op=ALU.add),
                reads=[A("mT"), A("hh", gq)], writes=[A("mgb", gq)])
        yield
        wo_sl = [(load_slab(f"wo{hf}a"), load_slab(f"wo{hf}b")) for hf in range(2)]
        pe(lambda e: [e.transpose(psb[TA][:, c * 128:(c + 1) * 128], mgb[:, c * 128:(c + 1) * 128], ident[:]) for c in range(8)][-1],
           reads=[A("mgb", q) for q in range(4)] + ["ident"], writes=[PS(TA)])
        act(lambda e: e.activation(mT[:].rearrange("p c t -> p (c t)"), psb[TA][:, 0:1024], AF.Copy), reads=[PS(TA)], writes=[A("mT")])
        yield
        for hf in range(2):
            (wa, kwa), (wb, kwb) = wo_sl[hf]
            ob = TB if hf == 0 else TC
            pe(lambda e, wa=wa, wb=wb, ob=ob: [e.matmul(ps[ob][:], mT[:, c, :], (wa if c < 4 else wb)[:, c % 4, :],
                                                        start=(c == 0), stop=(c == 7)) for c in range(8)][-1],
               reads=[kwa, kwb, A("mT")], writes=[PS(ob)])
            dve(lambda e, hf=hf, ob=ob: e.tensor_tensor(out=x_sb[:, gi, hf * 512:(hf + 1) * 512], in0=ps[ob][:],
                                                        in1=x_sb[:, gi, hf * 512:(hf + 1) * 512], op=ALU.add),
                reads=[PS(ob), ("x", gi)], writes=[("x", gi)])
        yield

    def tiles(g, phase):
        if phase == 1:
            for i in range(TPG):
                yield from mixer_tile(g, i, phase)
            return
        gens = [mixer_tile(g, i, 2) for i in range(TPG)]
        for _ in range(4):
            next(gens[0], None)
            yield
        for i in range(TPG):
            for _ in range(4):
                next(gens[i], None)
                yield
                if i + 1 < TPG:
                    next(gens[i + 1], None)
                    yield

    def run_all(gen):
        for _ in gen:
            pass

    out_toks = []

    def store_group(g):
        t = S.add("pool", lambda e: e.dma_start(
            out=y_d.ap()[g * G:(g + 1) * G, :].rearrange("(i p) d -> p i d", p=128),
            in_=x_sb[:, g * TPG:(g + 1) * TPG, :]),
            reads=[("x", g * TPG + i) for i in range(TPG)], writes=[("y", g)], dma=True)
        out_toks.append(t)

    def final_norm(g):
        load_bc("fing_bc")
        group_rstd(range(TPG), RMS_EPS, None, lambda ii, c, g=g: x_sb[:, g * TPG + ii, c * 512:(c + 1) * 512], 2,
                   lambda ii, g=g: [("x", g * TPG + ii)])
        rms_rstd(TPG)
        for ii in range(TPG):
            gi = g * TPG + ii
            dve(lambda e, ii=ii, gi=gi: e.scalar_tensor_tensor(x_sb[:, gi, :], x_sb[:, gi, :], rstd[:, ii:ii + 1], bcb[0][:],
                                                               op0=ALU.mult, op1=ALU.mult),
                reads=[("x", gi), "rstd", ("bcb", 0)], writes=[("x", gi)])

    if stage < 2:
        emit_conv(conv_order[n_early:])
        for g in range(NG):
            ffn(g, "f1", 0)
    else:
        ffn(0, "f1", 0, hoist=(lambda: norm_pre(1, alt=True),
                               lambda: norm_T_gen(1, 0, banks=(5, 6), alt=True, pre=False)))

        def p1_bg(g):
            yield from mixer_proj(g, 1)
            yield from tiles(g, 1)
        bgq = []
        for g in range(NG):
            if g == 2:
                dve(lambda e: e.memset(dummy[:, 1:2], 0.0), writes=["p1mark"])
                emit_conv(conv_order[n_early:n_mid], extra_reads=["p1mark"])
            if g + 1 < NG:
                hz = None
                if g + 2 < NG:
                    hz = (lambda g=g: norm_pre(g + 2, alt=True),
                          lambda g=g: norm_T_gen(g + 2, 0, banks=(5, 6), alt=True, pre=False))
                bgq.append(p1_bg(g))
                ffn(g + 1, "f1", 0, bg=bgq, rate=1, rateB=2, skip_norm=True, hoist=hz, drain=False)
            else:
                bgq.append(p1_bg(g))
                pump(bgq, 100000)
        if stage >= 3:
            for _ in norm_T_gen(0, 1):
                pass
        S.add("sp", lambda e: e.dma_start(out=cc_in.ap()[:, 0:8 * 257], in_=St[:].rearrange("p a b -> p (a b)")),
              reads=[("St", h) for h in range(H)], writes=["ccin0"], dma=True)
        S.add("sp", lambda e: e.dma_start(out=cc_in.ap()[:, 8 * 257:SPAY], in_=hist[:].rearrange("p a b -> p (a b)")),
              reads=[("hist", c) for c in range(16)], writes=["ccin1"], dma=True)
        S.add("pool", lambda e: e.collective_compute("AllGather", ALU.bypass,
                                                     replica_groups=[[0, 1], [2, 3], [4, 5], [6, 7]],
                                                     ins=[cc_in.ap().opt()], outs=[cc_out.ap().opt()]),
              reads=["ccin0", "ccin1"], writes=["ccout"], cc=True)
        fence_x()
        p1k = [k_ for k_ in set(S.last_w) | set(S.readers)
               if (isinstance(k_, tuple) and (k_[0] == "xnT1" or (k_[0] == "A" and len(k_) > 1 and k_[1] in ("pre1", "acc1"))))]
        p2k = [A(n, i, s) for n in ("ogm", "ugs", "vs") for i in range(TPG) for s in range(2)]
        S.add("dve", lambda e: e.memset(dummy[:, 0:1], 0.0), reads=p1k, writes=p2k + ["dummy"])
        S.add("sp", lambda e: e.dma_start(out=xch, in_=cc_out.ap()[0:128, :]), reads=["ccout"], writes=[A("xch")], dma=True)
        dve(lambda e: e.tensor_scalar(St[:].rearrange("p a b -> p (a b)"), xch[:, 0:8 * 257], sel[:, 0:1], None, op0=ALU.mult),
            reads=[A("xch"), "sel"], writes=[("St", h) for h in range(H)])
        dve(lambda e: e.tensor_scalar(hist[:].rearrange("p a b -> p (a b)"), xch[:, 8 * 257:SPAY], sel[:, 0:1], None, op0=ALU.mult),
            reads=[A("xch"), "sel"], writes=[("hist", c) for c in range(16)])
        for h in range(H):
            act(lambda e, h=h: e.activation(Stb[:, 2 * h:2 * h + 2, :], St[:, 2 * h:2 * h + 2, :], AF.Copy),
                reads=[("St", h)], writes=[("Stb", h)])
    if stage >= 2:
        pass
    if stage >= 3:
        pending = [None]

        def finish_prev():
            if pending[0] is not None:
                gp = pending[0]
                pending[0] = None
                if stage >= 5:
                    final_norm(gp)
                store_group(gp)
        prenormed = (stage >= 2)
        for g in range(NG):
            def after_norm(g=g):
                finish_prev()
                if g >= 1 and stage >= 4:
                    norm_pre(g - 1, alt=2)
            run_all(mixer_proj(g, 2, after_norm=after_norm, skip_norm=prenormed, stats_ready=(g == 1)))
            if g == 0:
                norm_pre(1, alt=2)
            if g == 0:
                emit_conv(conv_order[n_mid:])
            prenormed = False
            def p2_bg(g=g):
                yield from tiles(g, 2)
            if g == 0 or stage < 4:
                run_all(p2_bg())
            else:
                hz = None
                if g + 1 < NG:
                    hz = (lambda g=g: norm_pre(g + 1, alt=True),
                          lambda g=g: norm_T_gen(g + 1, 1, banks=(5, 6), alt=True, pre=False))
                    prenormed = True
                ffn(g - 1, "f2", 2, bg=p2_bg(), rate=1, rateB=1, hoist=hz, stats_ready=True)
                pending[0] = g - 1
        finish_prev()
        if stage >= 4:
            ffn(NG - 1, "f2", 2)
            if stage >= 5:
                final_norm(NG - 1)
            store_group(NG - 1)
        else:
            for g in range(NG):
                store_group(g)
    else:
        for g in range(NG):
            store_group(g)
    S.emit(final_waits=out_toks)
    return nc


_CACHE = {}


def _host_inputs(inp):
    f32 = lambda a: np.ascontiguousarray(np.asarray(a, dtype=np.float32))
    x = f32(inp["x"])
    common = {
        "ffn1_w_in": f32(inp["ffn1_w_in"][0]), "ffn1_w_out": f32(inp["ffn1_w_out"][0]),
        "ffn2_w_in": f32(inp["ffn2_w_in"][0]), "ffn2_w_out": f32(inp["ffn2_w_out"][0]),
        "w_mix_in": f32(inp["w_mix_in"][0]), "w_mix_out": f32(inp["w_mix_out"][0]),
    }
    colmaj = lambda v: np.asarray(v, np.float32).reshape(DC, 128).T
    common["gcols"] = f32(np.concatenate([colmaj(inp["ffn1_norm_g"][0]), colmaj(inp["mix_norm_g"][0]),
                                          colmaj(inp["ffn2_norm_g"][0])], axis=1))
    cw = np.asarray(inp["mlstm_conv_w"][0], np.float32)
    cb = np.asarray(inp["mlstm_conv_b"][0], np.float32)
    convp = np.concatenate([cw.reshape(4, 16, 128).transpose(2, 1, 0), cb.reshape(16, 128).T[:, :, None]], axis=2)
    common["convp"] = f32(convp.reshape(128, 80))
    bif = np.concatenate([np.asarray(inp["mlstm_b_igate"][0], np.float32), np.asarray(inp["mlstm_b_fgate"][0], np.float32)])
    common["bif"] = f32(np.broadcast_to(bif[None, :], (128, 8)))
    common["bs"] = f32(np.asarray(inp["sgu_b_s"][0], np.float32).T)
    common["wsT"] = f32(np.asarray(inp["sgu_w_s"][0], np.float32).transpose(2, 0, 1).reshape(128, 512))
    s_idx = np.arange(128)[:, None]
    t_idx = np.arange(128)[None, :]
    cst = np.stack([np.eye(128), (s_idx <= t_idx), ((t_idx // 64) >= (s_idx // 64))], axis=1).astype(np.float32)
    common["cst"] = f32(cst.reshape(128, 384))
    bc = lambda v: f32(np.broadcast_to(np.asarray(v, np.float32).reshape(1, D), (128, D)))
    common["headg_bc"] = bc(inp["mlstm_head_norm_g"][0])
    common["sgug_bc"] = bc(inp["sgu_norm_g"][0])
    common["sgub_bc"] = bc(inp["sgu_norm_b"][0])
    common["fing_bc"] = bc(inp["final_norm_g"])
    in_maps = []
    for c in range(NCORES):
        b, hf = c // 2, c % 2
        m = dict(common)
        m["x"] = f32(x[b, hf * NT:(hf + 1) * NT, :])
        m["sel"] = np.full((128, 1), float(hf), np.float32)
        in_maps.append(m)
    return in_maps


def kernel(**inputs):
    in_maps = _host_inputs(inputs)
    if STAGE not in _CACHE:
        _CACHE[STAGE] = build_program(STAGE)
    nc = _CACHE[STAGE]
    res = run_bass_kernel_spmd(nc, in_maps, core_ids=list(range(NCORES)))
    out = np.empty((4, 2 * NT, D), np.float32)
    for c in range(NCORES):
        out[c // 2, (c % 2) * NT:(c % 2 + 1) * NT, :] = res.results[c]["y"]
    return out
```

```python
import contextlib
import numpy as np
import concourse.bass as bass
import concourse.mybir as mybir
from concourse.bass_utils import run_bass_kernel_spmd

F32 = mybir.dt.float32
BF16 = mybir.dt.bfloat16
AF = mybir.ActivationFunctionType
ALU = mybir.AluOpType

NCORES = 8
NT = 2048
G = 512
NG = NT // G
TPG = G // 128
NTILE = NT // 128
D = 1024
DC = 8
DFF = 2816
JF = DFF // 128
H = 4
NPROJ = 8200
OQ, OK_, OV, OO, OI, OF_, OZ, OGM, OGS = 0, 1024, 2048, 3072, 4096, 4100, 4104, 6152, 7176
RMS_EPS = 1e-6
LN_EPS = 1e-5
SPAY = 8 * 257 + 48

POOL_CONV = False
PROFILE_LABELS = None
STAGE = 99


class Sched:
    ENGS = ("pe", "act", "dve", "pool", "sp")

    def __init__(self, nc):
        self.nc = nc
        self.ops = []
        self.last_w = {}
        self.readers = {}
        self.count = {e: 0 for e in self.ENGS}
        self.ndma = {"sp": 8, "pool": 3, "act": 4}
        self.dma_rr = {e: 0 for e in self.ndma}
        self.dma_cnt = {}

    def add(self, eng, fn, reads=(), writes=(), dma=False, cc=False):
        deps = set()
        reads = list(reads)
        writes = list(writes)
        for k in reads:
            if k in self.last_w:
                deps.add(self.last_w[k])
        for k in writes:
            if k in self.last_w:
                deps.add(self.last_w[k])
            deps.update(self.readers.get(k, ()))
        if eng == "pool" and dma and "ccout" in self.last_w:
            deps.add(self.last_w["ccout"])
        if cc:
            for r in range(self.ndma["pool"]):
                n_ = self.dma_cnt.get(("pool", r), 0)
                if n_:
                    deps.add((("dma", "pool", r), 16 * n_))
        if dma:
            R = self.ndma[eng]
            r = self.dma_rr[eng] % R
            self.dma_rr[eng] += 1
            n = self.dma_cnt.get((eng, r), 0) + 1
            self.dma_cnt[(eng, r)] = n
            sem = ("dma", eng, r)
            tok = (sem, 16 * n)
            if n > 1:
                deps.add((sem, 16 * (n - 1)))
            inc = 16
        elif cc:
            sem = ("cc",)
            n = self.dma_cnt.get(sem, 0) + 1
            self.dma_cnt[sem] = n
            tok = (sem, n)
            inc = 1
        else:
            self.count[eng] += 1
            sem = ("eng", eng)
            tok = (sem, self.count[eng])
            inc = 1
        lab = ""
        if PROFILE_LABELS is not None:
            import sys as _s
            fr = _s._getframe(1)
            names = []
            while fr is not None and len(names) < 4:
                if fr.f_code.co_name not in ("pe", "act", "dve", "<lambda>", "add"):
                    names.append("%s:%d" % (fr.f_code.co_name, fr.f_lineno))
                fr = fr.f_back
            lab = "<".join(names) + "|" + str(getattr(self, "note", ""))
        self.ops.append(dict(eng=eng, fn=fn, deps=deps, tok=tok, inc=inc, lab=lab))
        for k in writes:
            self.last_w[k] = tok
            self.readers[k] = []
        for k in reads:
            if k not in writes:
                self.readers.setdefault(k, []).append(tok)
        return tok

    def fence(self, eng, fn, pred):
        keys = [k for k in set(self.last_w) | set(self.readers) if pred(k)]
        return self.add(eng, fn, reads=(), writes=keys + ["dummy"])

    def emit(self, final_waits=()):
        nc = self.nc
        sem_names = sorted({op["tok"][0] for op in self.ops}, key=str)
        with contextlib.ExitStack() as st:
            sems = {}
            for s in sem_names:
                sems[s] = st.enter_context(nc.semaphore("s_" + "_".join(str(x) for x in s)))
            block = st.enter_context(nc.Block())
            ops = self.ops

            def run(engname, eng):
                waited = {}
                for op in ops:
                    if op["eng"] != engname:
                        continue
                    need = {}
                    for (s, v) in op["deps"]:
                        if s == ("eng", "pe") and engname == "pe":
                            continue
                        if v > need.get(s, 0):
                            need[s] = v
                    for s, v in sorted(need.items(), key=str):
                        if waited.get(s, 0) >= v:
                            continue
                        eng.wait_ge(sems[s], v)
                        waited[s] = v
                    if PROFILE_LABELS is not None and engname in ("pe", "dve", "act"):
                        cnt = [0]

                        class _P:
                            def __getattr__(s_, nm):
                                f_ = getattr(eng, nm)

                                def w(*a, **kw):
                                    cnt[0] += 1
                                    return f_(*a, **kw)
                                return w
                        ins = op["fn"](_P())
                        PROFILE_LABELS.append((engname, op["lab"], cnt[0]))
                    else:
                        ins = op["fn"](eng)
                    ins.then_inc(sems[op["tok"][0]], op["inc"])
                if engname == "sp":
                    need = {}
                    for (s, v) in final_waits:
                        if v > need.get(s, 0):
                            need[s] = v
                    for s, v in sorted(need.items(), key=str):
                        eng.wait_ge(sems[s], v)

            @block.sync
            def _(e):
                run("sp", e)

            @block.scalar
            def _(e):
                run("act", e)

            @block.vector
            def _(e):
                run("dve", e)

            @block.tensor
            def _(e):
                run("pe", e)

            @block.gpsimd
            def _(e):
                run("pool", e)


def build_program(stage=99):
    nc = bass.Bass("TRN2", target_bir_lowering=False)
    S = Sched(nc)

    def dram_in(name, shape):
        return nc.dram_tensor(name, list(shape), F32, kind="ExternalInput")

    x_d = dram_in("x", [NT, D])
    w_d = {
        "f1i": dram_in("ffn1_w_in", [D, 2 * DFF]), "f1o": dram_in("ffn1_w_out", [DFF, D]),
        "f2i": dram_in("ffn2_w_in", [D, 2 * DFF]), "f2o": dram_in("ffn2_w_out", [DFF, D]),
        "mi": dram_in("w_mix_in", [D, NPROJ]), "mo": dram_in("w_mix_out", [D, D]),
    }
    gcols_d = dram_in("gcols", [128, 24])
    convp_d = dram_in("convp", [128, 16 * 5])
    bif_d = dram_in("bif", [128, 8])
    bs_d = dram_in("bs", [128, 4])
    wsT_d = dram_in("wsT", [128, 4 * 128])
    cst_d = dram_in("cst", [128, 3 * 128])
    sel_d = dram_in("sel", [128, 1])
    bc_d = {k: dram_in(k, [128, D]) for k in ("headg_bc", "sgug_bc", "sgub_bc", "fing_bc")}
    y_d = nc.dram_tensor("y", [NT, D], F32, kind="ExternalOutput")

    cc_in = nc.dram_tensor("cc_in", [128, SPAY], F32)
    cc_out = nc.dram_tensor("cc_out", [256, SPAY], F32)

    JGN = [2, 2, 2, 2, 2, 1, 2, 2, 2, 2, 2, 1]
    JG0 = [0, 2, 4, 6, 8, 10, 11, 13, 15, 17, 19, 21]
    JON = [4, 4, 3, 4, 4, 3]
    JO0 = [0, 4, 8, 11, 15, 19]
    SLOT = 2048
    slabs = {}

    def def_slab(name, src, a, w, sc, idx):
        assert a * w <= SLOT, (name, a, w)
        slabs[name] = dict(src=src, a=a, w=w, dst=sc[idx][:, 0:a * w], key=("wsc", name))

    for f in ("f1", "f2"):
        wi = w_d[f + "i"].ap().rearrange("(dc p) c -> p dc c", p=128)
        wo = w_d[f + "o"].ap().rearrange("(j p) c -> p j c", p=128)
        sc_i = nc.dram_tensor(f + "i_sc", [24, 128, SLOT], BF16)
        sc_o = nc.dram_tensor(f + "o_sc", [12, 128, SLOT], BF16)
        for s in range(12):
            j0, nj = JG0[s], JGN[s]
            def_slab(f"{f}g{s}", wi[:, :, j0 * 128:(j0 + nj) * 128], 8, nj * 128, sc_i, 2 * s)
            def_slab(f"{f}u{s}", wi[:, :, DFF + j0 * 128:DFF + (j0 + nj) * 128], 8, nj * 128, sc_i, 2 * s + 1)
        for hf in range(2):
            for jg in range(6):
                j0, nj = JO0[jg], JON[jg]
                def_slab(f"{f}o{hf}_{jg}", wo[:, j0:j0 + nj, hf * 512:(hf + 1) * 512], nj, 512,
                         sc_o, hf * 6 + jg)
    wm = w_d["mi"].ap().rearrange("(dc p) c -> p dc c", p=128)
    sc_m = nc.dram_tensor("mi_sc", [33, 128, SLOT], BF16)
    mi_names = []
    for nm, off, n in (("q", OQ, 2), ("k", OK_, 2)):
        for s in range(n):
            for hx, sfx in enumerate("ab"):
                c0 = off + s * 512 + hx * 256
                def_slab(f"m{nm}{s}{sfx}", wm[:, :, c0:c0 + 256], 8, 256, sc_m, len(mi_names))
                mi_names.append(f"m{nm}{s}{sfx}")
    for nm, off, n in (("v", OV, 2), ("o", OO, 2), ("z", OZ, 4), ("gm", OGM, 2), ("gs", OGS, 2)):
        for s in range(n):
            for hx, sfx in enumerate("ab"):
                def_slab(f"m{nm}{s}{sfx}", wm[:, hx * 4:hx * 4 + 4, off + s * 512:off + (s + 1) * 512], 4, 512, sc_m,
                         len(mi_names))
                mi_names.append(f"m{nm}{s}{sfx}")
    def_slab("mif", wm[:, :, OI:OI + 8], 8, 8, sc_m, 32)
    wmo = w_d["mo"].ap().rearrange("(cc p) c -> p cc c", p=128)
    sc_mo = nc.dram_tensor("mo_sc", [4, 128, SLOT], BF16)
    for hf in range(2):
        for hx, sfx in enumerate("ab"):
            def_slab(f"wo{hf}{sfx}", wmo[:, hx * 4:hx * 4 + 4, hf * 512:(hf + 1) * 512], 4, 512, sc_mo, hf * 2 + hx)

    def ffn_order(f):
        o = []
        for jh in range(2):
            for s in range(6 * jh, 6 * jh + 6):
                o += [f"{f}g{s}", f"{f}u{s}"]
            o += [f"{f}o{hf}_{jg}" for hf in range(2) for jg in range(3 * jh, 3 * jh + 3)]
        return o

    def ab(names):
        return [n + x for n in names for x in "ab"]
    conv_order = ffn_order("f1")
    conv_order += ab(["mk0", "mk1", "mv0", "mv1"]) + ["mif"] + ab(["mq0", "mq1", "mo0", "mo1", "mz0", "mz1", "mz2", "mz3",
                                                                  "mgm0", "mgm1", "mgs0", "mgs1", "wo0", "wo1"])
    conv_order = list(dict.fromkeys(conv_order))
    conv_order += ffn_order("f2")
    assert set(conv_order) == set(slabs)

    def sb(name, shape, dt):
        return nc.alloc_sbuf_tensor("sb_" + name, list(shape), dt)

    x_sb = sb("x_sb", [128, NTILE, D], F32)
    NS = 8
    ring = [sb(f"ring{i}", [128, SLOT], BF16) for i in range(NS)]
    wif = sb("wif", [128, 64], BF16)
    xnT = sb("xnT", [128, DC, G], BF16)
    St = sb("St", [128, 8, 257], F32)
    Stb = sb("Stb", [128, 8, 257], BF16)
    bcb = [sb("bcb0", [128, D], F32)]
    gcols = sb("gcols", [128, 24], F32)
    convp = sb("convp", [128, 16, 5], F32)
    bif = sb("bif", [128, 8], F32)
    bs_sb = sb("bs_sb", [128, 4], F32)
    cst = sb("cst", [128, 3, 128], F32)
    ident = sb("ident", [128, 128], BF16)
    sel = sb("sel", [128, 1], F32)
    WmT = sb("WmT", [128, 4, 128], BF16)
    hist = sb("hist", [128, 16, 3], F32)
    stats = sb("stats", [128, 4, 12], F32)
    mv = sb("mv", [128, 4, 2], F32)
    msq = sb("msq", [128, 4], F32)
    rstd = sb("rstd", [128, 4], F32)
    stats2 = sb("stats2", [128, 4, 12], F32)
    mv2 = sb("mv2", [128, 4, 2], F32)
    msq2 = sb("msq2", [128, 4], F32)
    rstd2 = sb("rstd2", [128, 4], F32)
    stats3 = sb("stats3", [128, 4, 12], F32)
    mv3 = sb("mv3", [128, 4, 2], F32)
    msq3 = sb("msq3", [128, 4], F32)
    rstd3 = sb("rstd3", [128, 4], F32)
    gpre = sb("gpre", [128, TPG, 8], F32)
    sm = sb("sm", [128, 64], F32)
    lfb = sb("lfb", [128, TPG, 4], F32)
    dummy = sb("dummy", [128, 2], F32)
    ARENA = 82016
    arena = sb("arena", [128, ARENA // 2], BF16)
    apos = [0]

    def carve(shape, dt, base=None):
        n = int(np.prod(shape[1:]))
        nb = n * (4 if dt == F32 else 2)
        if base is not None:
            apos[0] = base
        off = apos[0]
        apos[0] = off + ((nb + 31) // 32) * 32
        assert apos[0] <= ARENA, (apos[0], ARENA)
        v = arena[:, off // 2:off // 2 + nb // 2]
        if dt == F32:
            v = v.bitcast(F32)
        if len(shape) == 3:
            v = v.rearrange("p (a b) -> p a b", a=shape[1])
        elif len(shape) == 4:
            v = v.rearrange("p (a b c) -> p a b c", a=shape[1], b=shape[2])
        return v

    qT = carve([128, 8, G], BF16, base=0)
    kT = carve([128, 8, G], BF16)
    v1 = carve([128, TPG, H, 258], BF16)
    ogm = carve([128, TPG, D], BF16)
    ugs = carve([128, TPG, D], BF16)
    vs = carve([128, TPG, D], BF16)
    ktok = carve([128, D], BF16)
    EB = carve([128, 4, 128], F32)
    qs = carve([128, 8, 128], BF16)
    PT = carve([128, 4, 128], BF16)
    v1w = carve([128, H, 258], BF16)
    hh = carve([128, D], F32)
    mgb = carve([128, D], BF16)
    mT = carve([128, 8, 128], BF16)
    XB = apos[0]
    hT = carve([128, 11, G], BF16, base=XB)
    sg = carve([128, 2, G], F32)
    pre = carve([128, 2, 516], F32, base=XB)
    acc = carve([128, 2, G], F32)
    tg = carve([128, 2, G], F32)
    xch = carve([128, SPAY], F32, base=XB)
    apos[0] = XB + 15360
    xs_bf = sb("xs_bf", [128, D], BF16)

    ps = [nc.alloc_psum_tensor(f"ps{i}", [128, 512], F32) for i in range(8)]
    psb = [p.bitcast(BF16) for p in ps]

    def PS(b):
        return ("ps", b)

    is_arena = lambda k: isinstance(k, tuple) and len(k) > 1 and k[0] == "A" and k[1] in ("hT", "sg", "pre", "acc", "tg", "xch")

    def A(*k):
        return ("A",) + tuple(k)

    ring_pos = [0]

    def load_slab(name):
        sl = slabs[name]
        slot = ring_pos[0] % NS
        ring_pos[0] += 1
        n = sl["a"] * sl["w"]
        S.add("sp", lambda e, slot=slot, n=n, sl=sl: e.dma_start(out=ring[slot][:, 0:n], in_=sl["dst"]),
              reads=[sl["key"]], writes=[("ring", slot)], dma=True)
        return ring[slot][:, 0:n].rearrange("p (a w) -> p a w", a=sl["a"]), ("ring", slot)

    def dve(fn, reads=(), writes=()):
        return S.add("dve", fn, reads, writes)

    def act(fn, reads=(), writes=()):
        return S.add("act", fn, reads, writes)

    def pe(fn, reads=(), writes=()):
        return S.add("pe", fn, reads, writes)

    S.add("pool", lambda e: e.memset(St[:], 0.0), writes=["St"])
    S.add("pool", lambda e: e.memset(hist[:], 0.0), writes=["hist"])
    S.add("pool", lambda e: e.memset(dummy[:], 0.0), writes=["dummy"])
    eps_t = sb("eps_t", [128, 2], F32)
    S.add("pool", lambda e: e.memset(eps_t[:, 0:1], RMS_EPS), writes=["eps"])
    S.add("pool", lambda e: e.memset(eps_t[:, 1:2], LN_EPS), writes=["eps"])
    def emit_conv(names, extra_reads=()):
        for name in names:
            sl = slabs[name]
            S.add("pool", lambda e, sl=sl: e.dma_start(
                out=sl["dst"].rearrange("p (a w) -> p a w", a=sl["a"]), in_=sl["src"]),
                reads=list(extra_reads), writes=[sl["key"]], dma=True)
    n_early = conv_order.index("mo0a")
    n_mid = conv_order.index("f2g0")
    def load_x(g):
        S.add("sp", lambda e, g=g: e.dma_start(
            out=x_sb[:, g * TPG:(g + 1) * TPG, :],
            in_=x_d.ap()[g * G:(g + 1) * G, :].rearrange("(i p) d -> p i d", p=128)),
            writes=[("x", g * TPG + i) for i in range(TPG)], dma=True)
    load_x(0)
    emit_conv(conv_order[:2], extra_reads=[("x", 0)])
    emit_conv(conv_order[2:n_early])
    for g in range(1):
        if g == 0:
            for dst, src, key in ((gcols, gcols_d, "gcols"), (cst, cst_d, "cst"), (bif, bif_d, "bif"),
                                  (convp, convp_d, "convp"), (bs_sb, bs_d, "bs"), (sel, sel_d, "sel")):
                d2 = dst[:] if len(dst.shape) == 2 else dst[:].rearrange("p a b -> p (a b)")
                S.add("sp", lambda e, d2=d2, src=src: e.dma_start(out=d2, in_=src.ap()), writes=[key], dma=True)
    dve(lambda e: e.tensor_copy(out=ident[:], in_=cst[:, 0, :]), reads=["cst"], writes=["ident"])
    S.add("sp", lambda e: e.dma_start(out=hh[:, 0:512], in_=wsT_d.ap()), writes=[A("hh", 0), A("hh", 1)], dma=True)
    dve(lambda e: e.tensor_tensor(out=WmT[:], in0=hh[:, 0:512].rearrange("p (g t) -> p g t", g=4),
                                  in1=cst[:, 2:3, :].broadcast_to([128, 4, 128]), op=ALU.mult),
        reads=[A("hh", 0), A("hh", 1), "cst"], writes=["WmT"])

    def group_rstd(tiles, eps, out_ap, src_fn, nper, keys_r):
        n = len(tiles)
        for ii in range(n):
            for c in range(nper):
                dve(lambda e, ii=ii, c=c: e.bn_stats(stats[:, ii, c * 6:(c + 1) * 6], src_fn(ii, c)),
                    reads=keys_r(ii), writes=[("stats", ii, c)])
            dve(lambda e, ii=ii: e.bn_aggr(mv[:, ii, :], stats[:, ii, 0:6 * nper]),
                reads=[("stats", ii, c) for c in range(nper)], writes=[("mv", ii)])
        return n

    def rms_rstd(n):
        dve(lambda e: e.scalar_tensor_tensor(msq[:, 0:n], mv[:, 0:n, 0], 1.0, mv[:, 0:n, 0], op0=ALU.mult, op1=ALU.mult),
            reads=[("mv", i) for i in range(n)], writes=["msq"])
        dve(lambda e: e.tensor_tensor(out=msq[:, 0:n], in0=msq[:, 0:n], in1=mv[:, 0:n, 1], op=ALU.add),
            reads=["msq"] + [("mv", i) for i in range(n)], writes=["msq"])
        act(lambda e: e.activation(rstd[:, 0:n], msq[:, 0:n], AF.Ln, bias=eps_t[:, 0:1]), reads=["msq", "eps"], writes=["rstd"])
        act(lambda e: e.activation(rstd[:, 0:n], rstd[:, 0:n], AF.Exp, scale=-0.5), reads=["rstd"], writes=["rstd"])


    def norm_pre(g, alt=False):
        for _ in norm_pre_gen(g, alt):
            pass

    def norm_pre_gen(g, alt=False):
        st_, mv_, ms_, rs_, sfx = [(stats, mv, msq, rstd, ""), (stats2, mv2, msq2, rstd2, "2"),
                                   (stats3, mv3, msq3, rstd3, "3")][int(alt)]
        n = TPG
        for ii in range(n):
            for c in range(2):
                dve(lambda e, ii=ii, c=c: e.bn_stats(st_[:, ii, c * 6:(c + 1) * 6], x_sb[:, g * TPG + ii, c * 512:(c + 1) * 512]),
                    reads=[("x", g * TPG + ii)], writes=[("stats" + sfx, ii, c)])
            dve(lambda e, ii=ii: e.bn_aggr(mv_[:, ii, :], st_[:, ii, 0:12]),
                reads=[("stats" + sfx, ii, c) for c in range(2)], writes=[("mv" + sfx, ii)])
            yield
        dve(lambda e: e.scalar_tensor_tensor(ms_[:, 0:n], mv_[:, 0:n, 0], 1.0, mv_[:, 0:n, 0], op0=ALU.mult, op1=ALU.mult),
            reads=[("mv" + sfx, i) for i in range(n)], writes=["msq" + sfx])
        dve(lambda e: e.tensor_tensor(out=ms_[:, 0:n], in0=ms_[:, 0:n], in1=mv_[:, 0:n, 1], op=ALU.add),
            reads=["msq" + sfx] + [("mv" + sfx, i) for i in range(n)], writes=["msq" + sfx])
        act(lambda e: e.activation(rs_[:, 0:n], ms_[:, 0:n], AF.Ln, bias=eps_t[:, 0:1]), reads=["msq" + sfx, "eps"], writes=["rstd" + sfx])
        act(lambda e: e.activation(rs_[:, 0:n], rs_[:, 0:n], AF.Exp, scale=-0.5), reads=["rstd" + sfx], writes=["rstd" + sfx])
        yield

    def norm_T_gen(g, gsel, dst=None, kn="xnT", banks=(0, 1), alt=False, pre=True):
        dst = xnT if dst is None else dst
        rs_, sfx = [(rstd, ""), (rstd2, "2"), (rstd3, "3")][int(alt)]
        if pre:
            norm_pre(g, alt)
            yield
            yield
        for ii in range(TPG):
            gi = g * TPG + ii
            act(lambda e, ii=ii, gi=gi: e.activation(xs_bf[:], x_sb[:, gi, :], AF.Copy, scale=rs_[:, ii:ii + 1]),
                reads=[("x", gi), "rstd" + sfx], writes=["xs_bf"])
            yield
            b = banks[ii % 2]
            pe(lambda e, b=b: [e.transpose(psb[b][:, dc * 128:(dc + 1) * 128], xs_bf[:, dc * 128:(dc + 1) * 128],
                                           ident[:]) for dc in range(DC)][-1],
               reads=["xs_bf", "ident"], writes=[PS(b)] + (PSD_KEYS if b == 7 else []))
            dve(lambda e, b=b, ii=ii: e.tensor_tensor(
                out=dst[:, :, ii * 128:(ii + 1) * 128],
                in0=psb[b][:, 0:1024].rearrange("p (d t) -> p d t", d=DC),
                in1=gcols[:, gsel * 8:(gsel + 1) * 8][:, :, None].broadcast_to([128, DC, 128]), op=ALU.mult),
                reads=[PS(b), "gcols"], writes=[(kn, ii)])
            yield

    def norm_T(g, gsel):
        for _ in norm_T_gen(g, gsel):
            pass

    PSD_KEYS = ([("psD", "bc")] + [("psD", "den", h) for h in range(H)] + [("psD", "n", h) for h in range(H)]
                + [("psD", "qh", si) for si in range(4)])
    xnT_all = [("xnT", ii) for ii in range(TPG)]

    def pump(bg, n):
        if bg is None:
            return
        if isinstance(bg, list):
            for _ in range(n):
                while bg:
                    if next(bg[0], "END") == "END":
                        bg.pop(0)
                    else:
                        break
                if not bg:
                    return
            return
        for _ in range(n):
            if next(bg, "END") == "END":
                return

    def fence_x():
        S.fence("dve", lambda e: e.memset(dummy[:, 0:1], 0.0), is_arena)

    def ffn(g, f, gsel, bg=None, rate=1, rateB=None, skip_norm=False, hoist=None, drain=True, stats_ready=False):
        rateB = rate if rateB is None else rateB
        if f == "f1" and g + 1 < NG:
            load_x(g + 1)
        fence_x()
        if not skip_norm:
            if stats_ready:
                for _ in norm_T_gen(g, gsel, alt=2, pre=False):
                    pass
            else:
                norm_T(g, gsel)
        hgen = None
        for jh in range(2):
            hpre = None
            if jh == 1 and hoist is not None:
                hpre = hoist[0]()
            jl = 0
            for s in range(6 * jh, 6 * jh + 6):
                wg, kg = load_slab(f"{f}g{s}")
                wu, ku = load_slab(f"{f}u{s}")
                for jj in range(JGN[s]):
                    bA, bB = (0, 1) if jl % 2 == 0 else (2, 3)

                    def mm(e, wg=wg, wu=wu, jj=jj, bA=bA, bB=bB):
                        for dc in range(DC):
                            e.matmul(ps[bA][:], wg[:, dc, jj * 128:(jj + 1) * 128], xnT[:, dc, :],
                                     start=(dc == 0), stop=(dc == DC - 1))
                        for dc in range(DC):
                            r = e.matmul(ps[bB][:], wu[:, dc, jj * 128:(jj + 1) * 128], xnT[:, dc, :],
                                         start=(dc == 0), stop=(dc == DC - 1))
                        return r
                    S.note = f"g{g}{f}A jh{jh} s{s} jj{jj}"
                    pe(mm, reads=[kg, ku] + xnT_all, writes=[PS(bA), PS(bB)])
                    act(lambda e, bA=bA, jl=jl: e.activation(sg[:, jl % 2, :], ps[bA][:], AF.Silu),
                        reads=[PS(bA)], writes=[A("sg", jl % 2)])
                    dve(lambda e, bB=bB, jl=jl: e.tensor_tensor(out=hT[:, jl, :], in0=sg[:, jl % 2, :], in1=ps[bB][:],
                                                                op=ALU.mult),
                        reads=[A("sg", jl % 2), PS(bB)], writes=[A("hT", jl)])
                    jl += 1
                    pump(hpre, 1)
                    pump(bg, rate)
            njh = jl
            pump(hpre, 10000)
            for hf in range(2):
                jl = 0
                for s in range(3 * jh, 3 * jh + 3):
                    wo, ko = load_slab(f"{f}o{hf}_{s}")
                    nj = JON[s]

                    def mm(e, wo=wo, jl=jl, nj=nj, njh=njh):
                        r = None
                        for jj in range(nj):
                            jx = jl + jj
                            for i in range(TPG):
                                r = e.matmul(ps[i][:], hT[:, jx, i * 128:(i + 1) * 128], wo[:, jj, :],
                                             start=(jx == 0), stop=(jx == njh - 1))
                        return r
                    S.note = f"g{g}{f}B jh{jh} hf{hf} s{s}"
                    pe(mm, reads=[ko] + [A("hT", jl + jj) for jj in range(nj)], writes=[PS(i) for i in range(TPG)])
                    jl += nj
                    if s == 3 * jh + 2:
                        for i in range(TPG):
                            gi = g * TPG + i
                            dve(lambda e, i=i, gi=gi, hf=hf: e.scalar_tensor_tensor(
                                x_sb[:, gi, hf * 512:(hf + 1) * 512], ps[i][:], 0.5, x_sb[:, gi, hf * 512:(hf + 1) * 512],
                                op0=ALU.mult, op1=ALU.add),
                                reads=[PS(i), ("x", gi)], writes=[("x", gi)])
                    if jh == 1 and hoist is not None:
                        if hgen is None:
                            hgen = hoist[1]()
                        pump(hgen, 1)
                    if not (jh == 1 and hf == 1 and s >= 3 * jh + 1):
                        pump(bg, rateB)
        if hgen is not None:
            pump(hgen, 10000)
        if drain:
            pump(bg, 10000)

    xnT1 = ogm.rearrange("p a b -> p (a b)").rearrange("p (d t) -> p d t", d=DC)
    pre1 = ugs.rearrange("p a b -> p (a b)")[:, 0:2 * 516 * 2].bitcast(F32).rearrange("p (a b) -> p a b", a=2)
    acc1 = vs.rearrange("p a b -> p (a b)")[:, 0:2 * G * 2].bitcast(F32).rearrange("p (a b) -> p a b", a=2)

    def conv_chunks(g, names, qk_off, dstT, phase):
        xin, kn = (xnT1, "xnT1") if phase == 1 else (xnT, "xnT")
        pr, ac = (pre1, acc1) if phase == 1 else (pre, acc)
        kp, ka = ("pre1", "acc1") if phase == 1 else ("pre", "acc")
        banks = (4, 5, 6, 7) if phase == 1 else (0, 1, 2, 3)
        xall = [(kn, ii) for ii in range(TPG)]
        c = 0
        pend = []
        for nm in ab(names):
            w, kw = load_slab(nm)
            for cc in range(2):
                ch = qk_off + c
                b = banks[c % 4]
                pe(lambda e, w=w, cc=cc, b=b: [e.matmul(ps[b][:], w[:, dc, cc * 128:(cc + 1) * 128], xin[:, dc, :],
                                                        start=(dc == 0), stop=(dc == DC - 1)) for dc in range(DC)][-1],
                   reads=[kw] + xall, writes=[PS(b)] + (PSD_KEYS if b == 7 else []))
                pb = c % 2
                veng = "pool" if (phase == 2 and POOL_CONV and c % 2 == 1) else "dve"
                act(lambda e, b=b, pb=pb: e.activation(pr[:, pb, 3:515], ps[b][:], AF.Copy),
                    reads=[PS(b)], writes=[A(kp, pb, "m")])
                S.add(veng, lambda e, pb=pb, ch=ch: e.tensor_copy(out=pr[:, pb, 0:3], in_=hist[:, ch, :]),
                      reads=[("hist", ch)], writes=[A(kp, pb, "h")])
                act(lambda e, b=b, pb=pb, ch=ch: e.activation(ac[:, pb, :], ps[b][:], AF.Identity, bias=convp[:, ch, 4:5],
                                                              scale=convp[:, ch, 3:4]),
                    reads=[PS(b), "convp"], writes=[A(ka, pb)])
                while pend:
                    pend.pop()()
                for jx in range(3):
                    S.add(veng, lambda e, pb=pb, ch=ch, jx=jx: e.scalar_tensor_tensor(
                        ac[:, pb, :], pr[:, pb, jx:jx + 512], convp[:, ch, jx:jx + 1], ac[:, pb, :], op0=ALU.mult, op1=ALU.add),
                        reads=[A(kp, pb, "m"), A(kp, pb, "h"), A(ka, pb)], writes=[A(ka, pb)])
                S.add(veng, lambda e, pb=pb, ch=ch: e.tensor_copy(out=hist[:, ch, :], in_=pr[:, pb, 512:515]),
                      reads=[A(kp, pb, "m")], writes=[("hist", ch)])
                pend.append(lambda c=c, pb=pb: act(lambda e: e.activation(dstT[:, c, :], ac[:, pb, :], AF.Silu),
                                                   reads=[A(ka, pb)], writes=[A("T", qk_off, c)]))
                c += 1
                yield
        while pend:
            pend.pop()()

    def tok_slab(g, nm, evac, phase=2):
        xin, kn = (xnT1, "xnT1") if phase == 1 else (xnT, "xnT")
        wa, kwa = load_slab(nm + "a")
        wb, kwb = load_slab(nm + "b")
        if phase == 1:
            base = 4
        else:
            tok_slab.flip ^= 1
            base = 4 * tok_slab.flip
        for i in range(TPG):
            b = base + i
            pe(lambda e, i=i, b=b: [e.matmul(ps[b][:], xin[:, dc, i * 128:(i + 1) * 128],
                                             (wa if dc < 4 else wb)[:, dc % 4, :],
                                             start=(dc == 0), stop=(dc == DC - 1)) for dc in range(DC)][-1],
               reads=[kwa, kwb, (kn, i)], writes=[PS(b)] + (PSD_KEYS if b == 7 else []))
            evac(i, b)
            yield
    tok_slab.flip = 0

    def load_bc(name):
        S.add("pool", lambda e: e.dma_start(out=bcb[0][:], in_=bc_d[name].ap()), writes=[("bcb", 0)], dma=True)

    def mixer_proj(g, phase, after_norm=None, skip_norm=False, stats_ready=False, side=None):
        xin, kn = (xnT1, "xnT1") if phase == 1 else (xnT, "xnT")
        if phase == 2:
            fence_x()
        dve(lambda e: e.memset(v1.rearrange("p a b c -> p (a b) c")[:, :, 256:258], 1.0), writes=[A("v1ones")])
        if skip_norm:
            pass
        elif phase == 1:
            yield from norm_T_gen(g, 1, dst=xnT1, kn="xnT1", banks=(4, 5))
        elif stats_ready:
            yield from norm_T_gen(g, 1, alt=2, pre=False)
        else:
            yield from norm_T_gen(g, 1)
        if after_norm is not None:
            after_norm()
        if phase == 2:
            for _ in conv_chunks(g, ["mq0", "mq1"], 0, qT, phase):
                pump(side, 1)
                yield
        elif g == NG - 1:
            for si, nm in enumerate(ab(["mq0", "mq1"])):
                w, kw = load_slab(nm)
                pe(lambda e, w=w, si=si: [e.matmul(ps[7][:, 32 + (2 * si + cc) * 4:32 + (2 * si + cc) * 4 + 3],
                                                   w[:, dc, cc * 128:(cc + 1) * 128], xnT1[:, dc, G - 3:G],
                                                   start=(dc == 0), stop=(dc == DC - 1))
                                          for cc in range(2) for dc in range(DC)][-1],
                   reads=[kw] + [("xnT1", ii) for ii in range(TPG)], writes=[("psD", "qh", si)])
            dve(lambda e: e.tensor_copy(out=hist[:, 0:8, :],
                                        in_=ps[7][:, 32:64].rearrange("p (c f) -> p c f", f=4)[:, :, 0:3]),
                reads=[("psD", "qh", si) for si in range(4)], writes=[("hist", c) for c in range(8)])
            yield
        for _ in conv_chunks(g, ["mk0", "mk1"], 8, kT, phase):
            if phase == 2:
                pump(side, 1)
            yield
        if phase == 2:
            pump(side, 10000)
        for s in range(2):
            yield from tok_slab(g, f"mv{s}", lambda i, b, s=s: act(
                lambda e: e.activation(v1[:, i, 2 * s:2 * s + 2, 0:256], ps[b][:].rearrange("p (h e) -> p h e", h=2), AF.Copy),
                reads=[PS(b)], writes=[A("v1", i, s)]), phase)
        S.add("sp", lambda e: e.dma_start(out=wif[:], in_=slabs["mif"]["dst"]), reads=[slabs["mif"]["key"]],
              writes=["wif"], dma=True)
        wifv = wif[:].rearrange("p (a w) -> p a w", a=8)
        for i in range(TPG):
            pe(lambda e, i=i: [e.matmul(ps[4][:, i * 8:(i + 1) * 8], xin[:, dc, i * 128:(i + 1) * 128], wifv[:, dc, :],
                                        start=(dc == 0), stop=(dc == DC - 1)) for dc in range(DC)][-1],
               reads=["wif", (kn, i)], writes=[PS(4)])
            dve(lambda e, i=i: e.tensor_tensor(out=gpre[:, i, :], in0=ps[4][:, i * 8:(i + 1) * 8], in1=bif[:], op=ALU.add),
                reads=[PS(4), "bif"], writes=[("gpre", i)])
        yield
        zz = gpre[:, :, 4:8]
        dve(lambda e: e.scalar_tensor_tensor(lfb[:], zz, -1.0, zz, op0=ALU.mult, op1=ALU.max),
            reads=[("gpre", i) for i in range(TPG)], writes=["lfb"])
        act(lambda e: e.activation(lfb[:], lfb[:], AF.Exp, scale=-1.0), reads=["lfb"], writes=["lfb"])
        act(lambda e: e.activation(lfb[:], lfb[:], AF.Ln, bias=1.0), reads=["lfb"], writes=["lfb"])
        dve(lambda e: e.scalar_tensor_tensor(lfb[:], zz, 0.0, lfb[:], op0=ALU.min, op1=ALU.subtract),
            reads=[("gpre", i) for i in range(TPG)] + ["lfb"], writes=["lfb"])
        if phase == 1:
            return
        for s in range(4):
            dst = ugs if s < 2 else vs
            yield from tok_slab(g, f"mz{s}", lambda i, b, s=s, dst=dst: act(
                lambda e: e.activation(dst[:, i, (s % 2) * 512:(s % 2 + 1) * 512], ps[b][:], AF.Gelu),
                reads=[PS(b)], writes=[A("ugs" if s < 2 else "vs", i, s % 2)]))
        ln = sgu_ln(g)
        for s in range(2):
            def evo(i, b, s=s):
                act(lambda e: e.activation(ogm[:, i, s * 512:(s + 1) * 512], ps[b][:], AF.Sigmoid),
                    reads=[PS(b)], writes=[A("ogm", i, s)])
                next(ln, None)
            yield from tok_slab(g, f"mo{s}", evo)
        for s in range(2):
            def ev(i, b, s=s):
                t = (i + s) % 2
                act(lambda e: e.activation(tg[:, t, :], ps[b][:], AF.Sigmoid), reads=[PS(b)], writes=[A("tg", t)])
                dve(lambda e: e.tensor_tensor(out=ugs[:, i, s * 512:(s + 1) * 512], in0=ugs[:, i, s * 512:(s + 1) * 512],
                                              in1=tg[:, t, :], op=ALU.mult),
                    reads=[A("tg", t), A("ugs", i, s)], writes=[A("ugs", i, s)])
                next(ln, None)
            yield from tok_slab(g, f"mgs{s}", ev)
        for _ in ln:
            pass
        load_bc("headg_bc")
        for s in range(2):
            def ev(i, b, s=s):
                t = (i + s) % 2
                act(lambda e: e.activation(tg[:, t, :], ps[b][:], AF.Sigmoid), reads=[PS(b)], writes=[A("tg", t)])
                dve(lambda e: e.tensor_tensor(out=tg[:, t, :], in0=tg[:, t, :], in1=bcb[0][:, s * 512:(s + 1) * 512], op=ALU.mult),
                    reads=[A("tg", t), ("bcb", 0)], writes=[A("tg", t)])
                dve(lambda e: e.tensor_tensor(out=ogm[:, i, s * 512:(s + 1) * 512], in0=ogm[:, i, s * 512:(s + 1) * 512],
                                              in1=tg[:, t, :], op=ALU.mult),
                    reads=[A("tg", t), A("ogm", i, s)], writes=[A("ogm", i, s)])
            yield from tok_slab(g, f"mgm{s}", ev)

    def sgu_ln(g):
        load_bc("sgug_bc")
        for ii in range(TPG):
            for c in range(2):
                dve(lambda e, ii=ii, c=c: e.bn_stats(stats2[:, ii, c * 6:(c + 1) * 6], vs[:, ii, c * 512:(c + 1) * 512]),
                    reads=[A("vs", ii, 0), A("vs", ii, 1)], writes=[("stats2", ii, c)])
            dve(lambda e, ii=ii: e.bn_aggr(mv2[:, ii, :], stats2[:, ii, 0:12]),
                reads=[("stats2", ii, c) for c in range(2)], writes=[("mv2", ii)])
            yield
        act(lambda e: e.activation(rstd2[:, 0:TPG], mv2[:, 0:TPG, 1], AF.Ln, bias=eps_t[:, 1:2]),
            reads=[("mv2", i) for i in range(TPG)] + ["eps"], writes=["rstd2"])
        act(lambda e: e.activation(rstd2[:, 0:TPG], rstd2[:, 0:TPG], AF.Exp, scale=-0.5), reads=["rstd2"], writes=["rstd2"])
        for i in range(TPG):
            dve(lambda e, i=i: e.scalar_tensor_tensor(vs[:, i, :], vs[:, i, :], mv2[:, i, 0:1], bcb[0][:], op0=ALU.subtract, op1=ALU.mult),
                reads=[A("vs", i, 0), A("vs", i, 1), ("mv2", i), ("bcb", 0)], writes=[A("vs", i, 0), A("vs", i, 1)])
            yield
        load_bc("sgub_bc")
        for i in range(TPG):
            dve(lambda e, i=i: e.scalar_tensor_tensor(vs[:, i, :], vs[:, i, :], rstd2[:, i:i + 1], bcb[0][:], op0=ALU.mult, op1=ALU.add),
                reads=[A("vs", i, 0), A("vs", i, 1), "rstd2", ("bcb", 0)], writes=[A("vs", i, 0), A("vs", i, 1)])
            yield

    TA, TB, TC, TD = 4, 5, 6, 7

    def mixer_tile(g, i, phase):
        gi = g * TPG + i
        tc = slice(i * 128, (i + 1) * 128)
        li = gpre[:, i, 0:4]
        lf = lfb[:, i, :]
        dl, wk, wkd, den, rd = (sm[:, 4 * k:4 * k + 4] for k in range(5))
        pe(lambda e: e.matmul(ps[TD][:, 0:4], cst[:, 1, :], lf, start=True, stop=True), reads=["lfb", "cst"], writes=[("psD", "bc"), PS(TD)])
        pe(lambda e: [e.matmul(ps[TA][:, h * 128:(h + 1) * 128], lf[:, h:h + 1].broadcast_to([128, 128]), cst[:, 1, :],
                               start=True, stop=True) for h in range(H)][-1],
           reads=["lfb", "cst"], writes=[PS(TA)])
        pe(lambda e: [e.transpose(psb[TB][:, c * 128:(c + 1) * 128], kT[:, c, tc], ident[:]) for c in range(8)][-1],
           reads=[A("T", 8, c) for c in range(8)] + ["ident"], writes=[PS(TB)])
        act(lambda e: e.activation(EB[:].rearrange("p h t -> p (h t)"), ps[TA][:], AF.Exp), reads=[PS(TA)], writes=[A("EB")])
        dve(lambda e: e.tensor_tensor(out=dl, in0=li, in1=ps[TD][:, 0:4], op=ALU.subtract), reads=[("gpre", i), ("psD", "bc")], writes=["dl"])
        act(lambda e: e.activation(wk, dl, AF.Exp), reads=["dl"], writes=["wk"])
        dve(lambda e: e.tensor_tensor(out=wkd, in0=wk, in1=EB[:, :, 127], op=ALU.mult), reads=["wk", A("EB")], writes=["wkd"])
        act(lambda e: e.activation(ktok[:], psb[TB][:, 0:1024], AF.Copy), reads=[PS(TB)], writes=[A("ktok")])
        for h in range(H):
            act(lambda e, h=h: e.activation(v1w[:, h, 0:257], v1[:, i, h, 0:257], AF.Copy, scale=wkd[:, h:h + 1]),
                reads=[A("v1", i, h // 2), A("v1ones"), "wkd"], writes=[A("v1w", h)])
        if phase == 2:
            for h in range(H):
                dve(lambda e, h=h: e.scalar_tensor_tensor(
                    qs[:, 2 * h:2 * h + 2, :], qT[:, 2 * h:2 * h + 2, tc], 1.0 / 16.0,
                    EB[:, h:h + 1, :].broadcast_to([128, 2, 128]), op0=ALU.mult, op1=ALU.mult),
                    reads=[A("T", 0, 2 * h), A("T", 0, 2 * h + 1), A("EB")], writes=[A("qs", h)])
        yield

        def state_upd(h, bk=TA):
            def mm(e, h=h):
                for c in range(2):
                    e.matmul(ps[bk][:, c * 256:(c + 1) * 256], ktok[:, (h * 2 + c) * 128:(h * 2 + c + 1) * 128], v1w[:, h, 0:256],
                             start=True, stop=True)
                for c in range(2):
                    r = e.matmul(ps[TD][:, 16 + h * 2 + c:17 + h * 2 + c], ktok[:, (h * 2 + c) * 128:(h * 2 + c + 1) * 128],
                                 v1w[:, h, 256:257], start=True, stop=True)
                return r
            pe(mm, reads=[A("ktok"), A("v1w", h)], writes=[PS(bk), ("psD", "n", h)])
            dve(lambda e, h=h: e.scalar_tensor_tensor(
                St[:, 2 * h:2 * h + 2, 0:256], St[:, 2 * h:2 * h + 2, 0:256], EB[:, h, 127:128],
                ps[bk][:].rearrange("p (c e) -> p c e", c=2), op0=ALU.mult, op1=ALU.add),
                reads=[PS(bk), A("EB"), ("St", h)], writes=[("St", h)])
            dve(lambda e, h=h: e.scalar_tensor_tensor(
                St[:, 2 * h:2 * h + 2, 256], St[:, 2 * h:2 * h + 2, 256], EB[:, h, 127:128],
                ps[TD][:, 16 + 2 * h:18 + 2 * h], op0=ALU.mult, op1=ALU.add),
                reads=[("psD", "n", h), A("EB"), ("St", h)], writes=[("St", h)])

        def stb_copy(h):
            act(lambda e, h=h: e.activation(Stb[:, 2 * h:2 * h + 2, :], St[:, 2 * h:2 * h + 2, :], AF.Copy),
                reads=[("St", h)], writes=[("Stb", h)])

        if phase == 1:
            for h in (0, 2):
                state_upd(h, TA)
                state_upd(h + 1, TC)
                yield
            return

        def out_mm(h):
            ob = TB if h < 2 else TC
            oc = (h % 2) * 256

            def mm(e, h=h, ob=ob, oc=oc):
                for c in range(2):
                    e.matmul(ps[ob][:, oc:oc + 256], qs[:, h * 2 + c, :], Stb[:, h * 2 + c, 0:256], start=(c == 0), stop=False)
                e.matmul(ps[ob][:, oc:oc + 256], PT[:, h, :], v1[:, i, h, 0:256], start=False, stop=True)
                for c in range(2):
                    e.matmul(ps[TD][:, 8 + h:9 + h], qs[:, h * 2 + c, :], Stb[:, h * 2 + c, 256:257], start=(c == 0), stop=False)
                return e.matmul(ps[TD][:, 8 + h:9 + h], PT[:, h, :], v1[:, i, h, 256:257], start=False, stop=True)
            pe(mm, reads=[A("qs", h), ("Stb", h), A("PT", h), A("v1", i, h // 2), A("v1ones")], writes=[PS(ob), ("psD", "den", h)])

        pe(lambda e: [e.matmul(ps[TC][:, h * 128:(h + 1) * 128], kT[:, h * 2 + c, tc], qs[:, h * 2 + c, :],
                               start=(c == 0), stop=(c == 1)) for h in range(H) for c in range(2)][-1],
           reads=[A("T", 8, c) for c in range(8)] + [A("qs", h) for h in range(H)], writes=[PS(TC)])
        state_upd(0)
        for h in range(H):
            dve(lambda e, h=h: e.scalar_tensor_tensor(PT[:, h, :], ps[TC][:, h * 128:(h + 1) * 128], wk[:, h:h + 1],
                                                      cst[:, 1, :], op0=ALU.mult, op1=ALU.mult),
                reads=[PS(TC), "wk", "cst"], writes=[A("PT", h)])
        yield
        def finish_heads(h0):
            hs_ = slice(h0, h0 + 2)
            kr, kd = ("rd", h0), ("den", h0)
            dve(lambda e: e.tensor_copy(out=rd[:, hs_], in_=ps[TD][:, 8 + h0:10 + h0]),
                reads=[("psD", "den", h0), ("psD", "den", h0 + 1)], writes=[kr])
            dve(lambda e: e.scalar_tensor_tensor(den[:, hs_], rd[:, hs_], -1.0, rd[:, hs_], op0=ALU.mult, op1=ALU.max), reads=[kr], writes=[kd])
            dve(lambda e: e.tensor_scalar(den[:, hs_], den[:, hs_], 1.0, None, op0=ALU.max), reads=[kd], writes=[kd])
            dve(lambda e: e.reciprocal(rd[:, hs_], den[:, hs_]), reads=[kd], writes=[kr])
            for h in (h0, h0 + 1):
                ob = TB if h < 2 else TC
                oc = (h % 2) * 256
                act(lambda e, h=h, ob=ob, oc=oc: e.activation(hh[:, h * 256:(h + 1) * 256], ps[ob][:, oc:oc + 256], AF.Copy,
                                                              scale=rd[:, h:h + 1]),
                    reads=[PS(ob), kr], writes=[A("hh", h)])
                dve(lambda e, h=h: e.bn_stats(stats[:, h, 0:6], hh[:, h * 256:(h + 1) * 256]), reads=[A("hh", h)], writes=[("stats", h, 0)])
                dve(lambda e, h=h: e.bn_aggr(mv[:, h, :], stats[:, h, 0:6]), reads=[("stats", h, 0)], writes=[("mv", h)])

        out_mm(0)
        out_mm(1)
        state_upd(1)
        stb_copy(0)
        stb_copy(1)
        finish_heads(0)
        yield
        out_mm(2)
        out_mm(3)
        state_upd(2)
        finish_heads(2)
        yield
        state_upd(3)
        stb_copy(2)
        stb_copy(3)
        act(lambda e: e.activation(rstd[:, 0:H], mv[:, 0:H, 1], AF.Ln, bias=eps_t[:, 1:2]),
            reads=[("mv", h) for h in range(H)] + ["eps"], writes=["rstd"])
        act(lambda e: e.activation(rstd[:, 0:H], rstd[:, 0:H], AF.Exp, scale=-0.5), reads=["rstd"], writes=["rstd"])
        for h in range(H):
            dve(lambda e, h=h: e.tensor_scalar(hh[:, h * 256:(h + 1) * 256], hh[:, h * 256:(h + 1) * 256], mv[:, h, 0:1],
                                               rstd[:, h:h + 1], op0=ALU.subtract, op1=ALU.mult),
                reads=[A("hh", h), ("mv", h), "rstd"], writes=[A("hh", h)])
        dve(lambda e: e.tensor_tensor(out=hh[:], in0=hh[:], in1=ogm[:, i, :], op=ALU.mult),
            reads=[A("hh", h) for h in range(H)] + [A("ogm", i, 0), A("ogm", i, 1)], writes=[A("hh", h) for h in range(H)])
        yield
        pe(lambda e: [e.matmul(ps[TB if gq < 2 else TC][:, (gq % 2) * 256:(gq % 2 + 1) * 256], WmT[:, gq, :],
                               vs[:, i, gq * 256:(gq + 1) * 256], start=True, stop=True) for gq in range(4)][-1],
           reads=["WmT", A("vs", i, 0), A("vs", i, 1)], writes=[PS(TB), PS(TC)])
        EBf = mT.rearrange("p c t -> p (c t)").bitcast(F32)
        for gq in range(4):
            bq = TB if gq < 2 else TC
            dve(lambda e, gq=gq, bq=bq: e.scalar_tensor_tensor(
                EBf[:, (gq % 2) * 256:(gq % 2 + 1) * 256], ps[bq][:, (gq % 2) * 256:(gq % 2 + 1) * 256],
                bs_sb[:, gq:gq + 1], ugs[:, i, gq * 256:(gq + 1) * 256], op0=ALU.add, op1=ALU.mult),
                reads=[PS(bq), "bs", A("ugs", i, gq // 2)], writes=[A("mT")])
            dve(lambda e, gq=gq: e.tensor_tensor(out=mgb[:, gq * 256:(gq + 1) * 256], in0=hh[:, gq * 256:(gq + 1) * 256],
                                                 in1=EBf[:, (gq % 2) * 256:(gq % 2 + 1) * 256], op=ALU.add),
                reads=[A("mT"), A("hh", gq)], writes=[A("mgb", gq)])
        yield
        wo_sl = [(load_slab(f"wo{hf}a"), load_slab(f"wo{hf}b")) for hf in range(2)]
        pe(lambda e: [e.transpose(psb[TA][:, c * 128:(c + 1) * 128], mgb[:, c * 128:(c + 1) * 128], ident[:]) for c in range(8)][-1],
           reads=[A("mgb", q) for q in range(4)] + ["ident"], writes=[PS(TA)])
        act(lambda e: e.activation(mT[:].rearrange("p c t -> p (c t)"), psb[TA][:, 0:1024], AF.Copy), reads=[PS(TA)], writes=[A("mT")])
        yield
        for hf in range(2):
            (wa, kwa), (wb, kwb) = wo_sl[hf]
            ob = TB if hf == 0 else TC
            pe(lambda e, wa=wa, wb=wb, ob=ob: [e.matmul(ps[ob][:], mT[:, c, :], (wa if c < 4 else wb)[:, c % 4, :],
                                                        start=(c == 0), stop=(c == 7)) for c in range(8)][-1],
               reads=[kwa, kwb, A("mT")], writes=[PS(ob)])
            dve(lambda e, hf=hf, ob=ob: e.tensor_tensor(out=x_sb[:, gi, hf * 512:(hf + 1) * 512], in0=ps[ob][:],
                                                        in1=x_sb[:, gi, hf * 512:(hf + 1) * 512], op=ALU.add),
                reads=[PS(ob), ("x", gi)], writes=[("x", gi)])
        yield

    def tiles(g, phase):
        if phase == 1:
            for i in range(TPG):
                yield from mixer_tile(g, i, phase)
            return
        gens = [mixer_tile(g, i, 2) for i in range(TPG)]
        for _ in range(4):
            next(gens[0], None)
            yield
        for i in range(TPG):
            for _ in range(4):
                next(gens[i], None)
                yield
                if i + 1 < TPG:
                    next(gens[i + 1], None)
                    yield

    def run_all(gen):
        for _ in gen:
            pass

    out_toks = []

    def store_group(g):
        t = S.add("pool", lambda e: e.dma_start(
            out=y_d.ap()[g * G:(g + 1) * G, :].rearrange("(i p) d -> p i d", p=128),
            in_=x_sb[:, g * TPG:(g + 1) * TPG, :]),
            reads=[("x", g * TPG + i) for i in range(TPG)], writes=[("y", g)], dma=True)
        out_toks.append(t)

    def store_tile(gi):
        t = S.add("pool", lambda e: e.dma_start(out=y_d.ap()[gi * 128:(gi + 1) * 128, :], in_=x_sb[:, gi, :]),
                  reads=[("x", gi)], writes=[("y", "tile", gi)], dma=True)
        out_toks.append(t)

    def final_norm_gen(g, split_store=False):
        load_bc("fing_bc")
        for ii in range(TPG):
            for c in range(2):
                dve(lambda e, ii=ii, c=c: e.bn_stats(stats[:, ii, c * 6:(c + 1) * 6], x_sb[:, g * TPG + ii, c * 512:(c + 1) * 512]),
                    reads=[("x", g * TPG + ii)], writes=[("stats", ii, c)])
            dve(lambda e, ii=ii: e.bn_aggr(mv[:, ii, :], stats[:, ii, 0:12]),
                reads=[("stats", ii, c) for c in range(2)], writes=[("mv", ii)])
            yield
        rms_rstd(TPG)
        yield
        for ii in range(TPG):
            gi = g * TPG + ii
            dve(lambda e, ii=ii, gi=gi: e.scalar_tensor_tensor(x_sb[:, gi, :], x_sb[:, gi, :], rstd[:, ii:ii + 1], bcb[0][:],
                                                               op0=ALU.mult, op1=ALU.mult),
                reads=[("x", gi), "rstd", ("bcb", 0)], writes=[("x", gi)])
            if split_store:
                store_tile(gi)
            yield

    def final_norm(g, split_store=False):
        for _ in final_norm_gen(g, split_store):
            pass

    if stage < 2:
        emit_conv(conv_order[n_early:])
        for g in range(NG):
            ffn(g, "f1", 0)
    else:
        ffn(0, "f1", 0, hoist=(lambda: norm_pre_gen(1, alt=True),
                               lambda: norm_T_gen(1, 0, banks=(5, 6), alt=True, pre=False)))

        def p1_bg(g):
            yield from mixer_proj(g, 1)
            yield from tiles(g, 1)
        bgq = []
        for g in range(NG):
            if g == 2:
                dve(lambda e: e.memset(dummy[:, 1:2], 0.0), writes=["p1mark"])
                emit_conv(conv_order[n_early:n_mid], extra_reads=["p1mark"])
            if g + 1 < NG:
                hz = None
                if g + 2 < NG:
                    hz = (lambda g=g: norm_pre_gen(g + 2, alt=True),
                          lambda g=g: norm_T_gen(g + 2, 0, banks=(5, 6), alt=True, pre=False))
                bgq.append(p1_bg(g))
                ffn(g + 1, "f1", 0, bg=bgq, rate=1, rateB=2, skip_norm=True, hoist=hz, drain=False)
            else:
                bgq.append(p1_bg(g))
                pump(bgq, 100000)
        if stage >= 3:
            for _ in norm_T_gen(0, 1):
                pass
        S.add("sp", lambda e: e.dma_start(out=cc_in.ap()[:, 0:8 * 257], in_=St[:].rearrange("p a b -> p (a b)")),
              reads=[("St", h) for h in range(H)], writes=["ccin0"], dma=True)
        S.add("sp", lambda e: e.dma_start(out=cc_in.ap()[:, 8 * 257:SPAY], in_=hist[:].rearrange("p a b -> p (a b)")),
              reads=[("hist", c) for c in range(16)], writes=["ccin1"], dma=True)
        S.add("pool", lambda e: e.collective_compute("AllGather", ALU.bypass,
                                                     replica_groups=[[0, 1], [2, 3], [4, 5], [6, 7]],
                                                     ins=[cc_in.ap().opt()], outs=[cc_out.ap().opt()]),
              reads=["ccin0", "ccin1"], writes=["ccout"], cc=True)
        fence_x()
        p1k = [k_ for k_ in set(S.last_w) | set(S.readers)
               if (isinstance(k_, tuple) and (k_[0] == "xnT1" or (k_[0] == "A" and len(k_) > 1 and k_[1] in ("pre1", "acc1"))))]
        p2k = [A(n, i, s) for n in ("ogm", "ugs", "vs") for i in range(TPG) for s in range(2)]
        S.add("dve", lambda e: e.memset(dummy[:, 0:1], 0.0), reads=p1k, writes=p2k + ["dummy"])
        S.add("sp", lambda e: e.dma_start(out=xch, in_=cc_out.ap()[0:128, :]), reads=["ccout"], writes=[A("xch")], dma=True)
        dve(lambda e: e.tensor_scalar(St[:].rearrange("p a b -> p (a b)"), xch[:, 0:8 * 257], sel[:, 0:1], None, op0=ALU.mult),
            reads=[A("xch"), "sel"], writes=[("St", h) for h in range(H)])
        dve(lambda e: e.tensor_scalar(hist[:].rearrange("p a b -> p (a b)"), xch[:, 8 * 257:SPAY], sel[:, 0:1], None, op0=ALU.mult),
            reads=[A("xch"), "sel"], writes=[("hist", c) for c in range(16)])
        for h in range(H):
            act(lambda e, h=h: e.activation(Stb[:, 2 * h:2 * h + 2, :], St[:, 2 * h:2 * h + 2, :], AF.Copy),
                reads=[("St", h)], writes=[("Stb", h)])
    if stage >= 2:
        pass
    if stage >= 3:
        pending = [None]

        def finish_prev_gen():
            if pending[0] is not None:
                gp = pending[0]
                pending[0] = None
                if stage >= 5:
                    yield from final_norm_gen(gp)
                store_group(gp)
                yield

        def finish_prev():
            for _ in finish_prev_gen():
                pass
        prenormed = (stage >= 2)
        for g in range(NG):
            def side_gen(g=g):
                yield from finish_prev_gen()
                if g >= 1 and stage >= 4:
                    yield from norm_pre_gen(g - 1, alt=2)
            run_all(mixer_proj(g, 2, skip_norm=prenormed, stats_ready=(g == 1), side=side_gen()))
            if g == 0:
                norm_pre(1, alt=2)
            if g == 0:
                emit_conv(conv_order[n_mid:])
            prenormed = False
            def p2_bg(g=g):
                yield from tiles(g, 2)
            if g == 0 or stage < 4:
                run_all(p2_bg())
            else:
                hz = None
                if g + 1 < NG:
                    hz = (lambda g=g: norm_pre_gen(g + 1, alt=True),
                          lambda g=g: norm_T_gen(g + 1, 1, banks=(5, 6), alt=True, pre=False))
                    prenormed = True
                ffn(g - 1, "f2", 2, bg=p2_bg(), rate=1, rateB=1, hoist=hz, stats_ready=True)
                pending[0] = g - 1
        if stage >= 4:
            ffn(NG - 1, "f2", 2, bg=finish_prev_gen())
            if stage >= 5:
                final_norm(NG - 1, split_store=True)
            else:
                store_group(NG - 1)
        else:
            finish_prev()
            for g in range(NG):
                store_group(g)
    else:
        for g in range(NG):
            store_group(g)
    S.emit(final_waits=out_toks)
    return nc


_CACHE = {}


def _host_inputs(inp):
    f32 = lambda a: np.ascontiguousarray(np.asarray(a, dtype=np.float32))
    x = f32(inp["x"])
    common = {
        "ffn1_w_in": f32(inp["ffn1_w_in"][0]), "ffn1_w_out": f32(inp["ffn1_w_out"][0]),
        "ffn2_w_in": f32(inp["ffn2_w_in"][0]), "ffn2_w_out": f32(inp["ffn2_w_out"][0]),
        "w_mix_in": f32(inp["w_mix_in"][0]), "w_mix_out": f32(inp["w_mix_out"][0]),
    }
    colmaj = lambda v: np.asarray(v, np.float32).reshape(DC, 128).T
    common["gcols"] = f32(np.concatenate([colmaj(inp["ffn1_norm_g"][0]), colmaj(inp["mix_norm_g"][0]),
                                          colmaj(inp["ffn2_norm_g"][0])], axis=1))
    cw = np.asarray(inp["mlstm_conv_w"][0], np.float32)
    cb = np.asarray(inp["mlstm_conv_b"][0], np.float32)
    convp = np.concatenate([cw.reshape(4, 16, 128).transpose(2, 1, 0), cb.reshape(16, 128).T[:, :, None]], axis=2)
    common["convp"] = f32(convp.reshape(128, 80))
    bif = np.concatenate([np.asarray(inp["mlstm_b_igate"][0], np.float32), np.asarray(inp["mlstm_b_fgate"][0], np.float32)])
    common["bif"] = f32(np.broadcast_to(bif[None, :], (128, 8)))
    common["bs"] = f32(np.asarray(inp["sgu_b_s"][0], np.float32).T)
    common["wsT"] = f32(np.asarray(inp["sgu_w_s"][0], np.float32).transpose(2, 0, 1).reshape(128, 512))
    s_idx = np.arange(128)[:, None]
    t_idx = np.arange(128)[None, :]
    cst = np.stack([np.eye(128), (s_idx <= t_idx), ((t_idx // 64) >= (s_idx // 64))], axis=1).astype(np.float32)
    common["cst"] = f32(cst.reshape(128, 384))
    bc = lambda v: f32(np.broadcast_to(np.asarray(v, np.float32).reshape(1, D), (128, D)))
    common["headg_bc"] = bc(inp["mlstm_head_norm_g"][0])
    common["sgug_bc"] = bc(inp["sgu_norm_g"][0])
    common["sgub_bc"] = bc(inp["sgu_norm_b"][0])
    common["fing_bc"] = bc(inp["final_norm_g"])
    in_maps = []
    for c in range(NCORES):
        b, hf = c // 2, c % 2
        m = dict(common)
        m["x"] = f32(x[b, hf * NT:(hf + 1) * NT, :])
        m["sel"] = np.full((128, 1), float(hf), np.float32)
        in_maps.append(m)
    return in_maps


def kernel(**inputs):
    in_maps = _host_inputs(inputs)
    if STAGE not in _CACHE:
        _CACHE[STAGE] = build_program(STAGE)
    nc = _CACHE[STAGE]
    res = run_bass_kernel_spmd(nc, in_maps, core_ids=list(range(NCORES)))
    out = np.empty((4, 2 * NT, D), np.float32)
    for c in range(NCORES):
        out[c // 2, (c % 2) * NT:(c % 2 + 1) * NT, :] = res.results[c]["y"]
    return out
```
